# Optimizing a Trainium2 kernel written in Bass

```python
import math
import jax, jax.numpy as jnp
from jax import lax
import numpy as np

D_MODEL = 2048
BATCH = 2
SEQ = 8192
DEPTH = 2
DEC_BATCH = 4
DEC_SEQ = 4096
PAST_LEN = 128

HEAD_DIM = 128
N_ATTN_HEADS = 8
ATTN_WIDTH = N_ATTN_HEADS * HEAD_DIM
QK_DIM = HEAD_DIM // 2
N_GATE_GROUPS = 8
GATE_GROUP_DIM = 128
GATE_WIDTH = N_GATE_GROUPS * GATE_GROUP_DIM
CHUNK = 128
MIX_WIDTH = ATTN_WIDTH + GATE_WIDTH
IN_WIDTH = 3 * ATTN_WIDTH + 2 * GATE_WIDTH
Q_BLOCK = 128
ROPE_THETA = 10000.0
N_RET_HEADS = 8
KEY_DIM = 256
HALF_KEY = KEY_DIM // 2
N_KEYS = 128
N_EXPERTS = N_KEYS * N_KEYS
TOPK = 16
TOKEN_BLOCK = 128
EPS = 1e-6

kernel_name = "hymba_style_diffattn_gmlp_peer_encoder"


def rmsnorm(x, g):
    xf = x.astype(jnp.float32)
    y = xf * lax.rsqrt(jnp.mean(xf * xf, axis=-1, keepdims=True) + EPS)
    return (y * g.astype(jnp.float32)).astype(x.dtype)


def layernorm(x, g, b):
    xf = x.astype(jnp.float32)
    mu = jnp.mean(xf, axis=-1, keepdims=True)
    xc = xf - mu
    var = jnp.mean(xc * xc, axis=-1, keepdims=True)
    y = xc * lax.rsqrt(var + EPS) * g.astype(jnp.float32) + b.astype(jnp.float32)
    return y.astype(x.dtype)


def rope_tables(seq):
    pos = jnp.arange(seq, dtype=jnp.float32)
    inv = ROPE_THETA ** (-jnp.arange(0, QK_DIM, 2, dtype=jnp.float32) / QK_DIM)
    ang = pos[:, None] * inv[None, :]
    return jnp.cos(ang), jnp.sin(ang)


def apply_rope(x, cos, sin):
    shape = (1, cos.shape[0]) + (1,) * (x.ndim - 3) + (cos.shape[1],)
    c = cos.reshape(shape)
    s = sin.reshape(shape)
    xf = x.astype(jnp.float32)
    x1, x2 = xf[..., : QK_DIM // 2], xf[..., QK_DIM // 2:]
    out = jnp.concatenate([x1 * c - x2 * s, x2 * c + x1 * s], axis=-1)
    return out.astype(x.dtype)


def lambda_init_fn(layer):
    return 0.8 - 0.6 * math.exp(-0.3 * layer)


def diff_attention(q, k, v, lam):
    B, S = q.shape[0], q.shape[1]
    nb = S // Q_BLOCK
    scale = QK_DIM ** -0.5
    qb = q.reshape(B, nb, Q_BLOCK, N_ATTN_HEADS, 2, QK_DIM).transpose(1, 0, 2, 3, 4, 5)

    def block(qblk):
        s = jnp.einsum('bqhcd,bkhcd->bchqk', qblk, k).astype(jnp.float32) * scale
        p = jax.nn.softmax(s, axis=-1)
        a = p[:, 0] - lam * p[:, 1]
        return jnp.einsum('bhqk,bkhd->bqhd', a.astype(v.dtype), v)

    out = lax.map(block, qb)
    return out.transpose(1, 0, 2, 3, 4).reshape(B, S, N_ATTN_HEADS, HEAD_DIM)


def spatial_gating(gu, gv, ln_g, ln_b, sw, sb, out_g):
    B, S = gu.shape[0], gu.shape[1]
    vg = gv.reshape(B, S, N_GATE_GROUPS, GATE_GROUP_DIM)
    vg = layernorm(vg, ln_g.reshape(N_GATE_GROUPS, GATE_GROUP_DIM), ln_b.reshape(N_GATE_GROUPS, GATE_GROUP_DIM))
    vc = vg.reshape(B, S // CHUNK, CHUNK, N_GATE_GROUPS, GATE_GROUP_DIM)
    mixed = jnp.einsum('gpq,bnqgc->bnpgc', sw, vc) + sb.T[None, None, :, :, None]
    mixed = mixed.reshape(B, S, N_GATE_GROUPS, GATE_GROUP_DIM)
    out = gu.reshape(B, S, N_GATE_GROUPS, GATE_GROUP_DIM) * mixed
    out = rmsnorm(out, out_g.reshape(N_GATE_GROUPS, GATE_GROUP_DIM))
    return out.reshape(B, S, GATE_WIDTH)


def mixer_layer(x, cos, sin, layer, norm_mix, w_in, lq1, lk1, lq2, lk2, subln,
                gate_ln_g, gate_ln_b, spatial_w, spatial_b, gate_out_norm, w_out):
    B, S, _ = x.shape
    h = rmsnorm(x, norm_mix)
    z = h @ w_in
    q = z[..., :ATTN_WIDTH].reshape(B, S, N_ATTN_HEADS, 2, QK_DIM)
    k = z[..., ATTN_WIDTH:2 * ATTN_WIDTH].reshape(B, S, N_ATTN_HEADS, 2, QK_DIM)
    v = z[..., 2 * ATTN_WIDTH:3 * ATTN_WIDTH].reshape(B, S, N_ATTN_HEADS, HEAD_DIM)
    g = jax.nn.gelu(z[..., 3 * ATTN_WIDTH:])
    gu, gv = g[..., :GATE_WIDTH], g[..., GATE_WIDTH:]

    q = apply_rope(q, cos, sin)
    k = apply_rope(k, cos, sin)
    lam_init = lambda_init_fn(layer)
    lam = (jnp.exp(jnp.sum(lq1.astype(jnp.float32) * lk1.astype(jnp.float32)))
           - jnp.exp(jnp.sum(lq2.astype(jnp.float32) * lk2.astype(jnp.float32)))
           + lam_init)
    attn = diff_attention(q, k, v, lam)
    attn = (rmsnorm(attn, subln) * (1.0 - lam_init)).astype(x.dtype).reshape(B, S, ATTN_WIDTH)

    gate = spatial_gating(gu, gv, gate_ln_g, gate_ln_b, spatial_w, spatial_b, gate_out_norm)
    return x + jnp.concatenate([attn, gate], axis=-1) @ w_out


def peer_layer(x, norm_ffn, w_query, sub_keys, expert_down, expert_up):
    B, S, D = x.shape
    h = rmsnorm(x, norm_ffn)
    nb = (B * S) // TOKEN_BLOCK
    hb = h.reshape(nb, TOKEN_BLOCK, D)

    def block(hx):
        q = (hx @ w_query).reshape(TOKEN_BLOCK, N_RET_HEADS, 2, HALF_KEY)
        s = jnp.einsum('thcd,chkd->thck', q, sub_keys).astype(jnp.float32)
        sv, si = lax.top_k(s, TOPK)
        cand = (sv[:, :, 0, :, None] + sv[:, :, 1, None, :]).reshape(TOKEN_BLOCK, N_RET_HEADS, TOPK * TOPK)
        cidx = (si[:, :, 0, :, None] * N_KEYS + si[:, :, 1, None, :]).reshape(TOKEN_BLOCK, N_RET_HEADS, TOPK * TOPK)
        fv, fi = lax.top_k(cand, TOPK)
        eidx = jnp.take_along_axis(cidx, fi, axis=-1)
        gates = jax.nn.softmax(fv, axis=-1)
        u = jnp.take(expert_down, eidx, axis=0)
        act = jax.nn.gelu(jnp.einsum('thkd,td->thk', u, hx))
        w = (gates * act.astype(jnp.float32)).astype(hx.dtype)
        vv = jnp.take(expert_up, eidx, axis=0)
        return jnp.einsum('thk,thkd->td', w, vv)

    out = lax.map(block, hb).reshape(B, S, D)
    return x + out


def trunk(x, norm_mix, w_in, lambda_q1, lambda_k1, lambda_q2, lambda_k2, subln,
          gate_ln_g, gate_ln_b, spatial_w, spatial_b, gate_out_norm, w_out,
          norm_ffn, w_query, sub_keys, expert_down, expert_up, norm_final):
    cos, sin = rope_tables(x.shape[1])
    for l in range(DEPTH):
        x = mixer_layer(x, cos, sin, l, norm_mix[l], w_in[l], lambda_q1[l], lambda_k1[l],
                        lambda_q2[l], lambda_k2[l], subln[l], gate_ln_g[l], gate_ln_b[l],
                        spatial_w[l], spatial_b[l], gate_out_norm[l], w_out[l])
        x = peer_layer(x, norm_ffn[l], w_query[l], sub_keys[l], expert_down[l], expert_up[l])
    return rmsnorm(x, norm_final)


def setup_inputs(seed: int = 0) -> dict:
    key = jax.random.key(seed)
    ks = jax.random.split(key, 24)
    f32 = jnp.float32
    nrm = lambda k, shape, std: jax.random.normal(k, shape, f32) * std
    return {
        "x_prompt": nrm(ks[0], (BATCH, SEQ, D_MODEL), 1.0),
        "x_sample": nrm(ks[1], (DEC_BATCH, DEC_SEQ, D_MODEL), 1.0),
        "norm_mix": 1.0 + nrm(ks[2], (DEPTH, D_MODEL), 0.02),
        "w_in": nrm(ks[3], (DEPTH, D_MODEL, IN_WIDTH), D_MODEL ** -0.5),
        "lambda_q1": nrm(ks[4], (DEPTH, QK_DIM), 0.1),
        "lambda_k1": nrm(ks[5], (DEPTH, QK_DIM), 0.1),
        "lambda_q2": nrm(ks[6], (DEPTH, QK_DIM), 0.1),
        "lambda_k2": nrm(ks[7], (DEPTH, QK_DIM), 0.1),
        "subln": 1.0 + nrm(ks[8], (DEPTH, HEAD_DIM), 0.02),
        "gate_ln_g": 1.0 + nrm(ks[9], (DEPTH, GATE_WIDTH), 0.02),
        "gate_ln_b": nrm(ks[10], (DEPTH, GATE_WIDTH), 0.02),
        "spatial_w": nrm(ks[11], (DEPTH, N_GATE_GROUPS, CHUNK, CHUNK), CHUNK ** -0.5),
        "spatial_b": 1.0 + nrm(ks[12], (DEPTH, N_GATE_GROUPS, CHUNK), 0.02),
        "gate_out_norm": 1.0 + nrm(ks[13], (DEPTH, GATE_WIDTH), 0.02),
        "w_out": nrm(ks[14], (DEPTH, MIX_WIDTH, D_MODEL), MIX_WIDTH ** -0.5),
        "norm_ffn": 1.0 + nrm(ks[15], (DEPTH, D_MODEL), 0.02),
        "w_query": nrm(ks[16], (DEPTH, D_MODEL, N_RET_HEADS * KEY_DIM), D_MODEL ** -0.5),
        "sub_keys": nrm(ks[17], (DEPTH, 2, N_RET_HEADS, N_KEYS, HALF_KEY), HALF_KEY ** -0.5),
        "expert_down": nrm(ks[18], (DEPTH, N_EXPERTS, D_MODEL), D_MODEL ** -0.5),
        "expert_up": nrm(ks[19], (DEPTH, N_EXPERTS, D_MODEL), (N_RET_HEADS * TOPK) ** -0.5),
        "norm_final": 1.0 + nrm(ks[20], (D_MODEL,), 0.02),
    }


def reference(x_prompt, x_sample, norm_mix, w_in, lambda_q1, lambda_k1, lambda_q2, lambda_k2,
              subln, gate_ln_g, gate_ln_b, spatial_w, spatial_b, gate_out_norm, w_out,
              norm_ffn, w_query, sub_keys, expert_down, expert_up, norm_final):
    y_prompt = trunk(x_prompt, norm_mix, w_in, lambda_q1, lambda_k1, lambda_q2, lambda_k2, subln,
                     gate_ln_g, gate_ln_b, spatial_w, spatial_b, gate_out_norm, w_out,
                     norm_ffn, w_query, sub_keys, expert_down, expert_up, norm_final)
    y_sample = trunk(x_sample, norm_mix, w_in, lambda_q1, lambda_k1, lambda_q2, lambda_k2, subln,
                     gate_ln_g, gate_ln_b, spatial_w, spatial_b, gate_out_norm, w_out,
                     norm_ffn, w_query, sub_keys, expert_down, expert_up, norm_final)
    return (y_prompt, y_sample)
```

```python
import math
from contextlib import ExitStack, contextmanager

import numpy as np
import ml_dtypes

import concourse.bass as bass
import concourse.mybir as mybir
from concourse.bass_utils import run_bass_kernel_spmd

F32 = mybir.dt.float32
BF16 = mybir.dt.bfloat16
AF = mybir.ActivationFunctionType
ALU = mybir.AluOpType
AX = mybir.AxisListType

NCORES = 8
TOK = 4096
D = 2048
NT = 8
TOKW = 8192
NTW = 16
INW = 5120
NE = 16384
EPS = 1e-6
NEG = -30000.0

ENGS = ("tensor", "vector", "scalar", "gpsimd", "sync")


class DSem:
    __slots__ = ("h", "cnt", "sw")

    def __init__(self, h):
        self.h = h
        self.cnt = 0
        self.sw = False


class Buf:
    __slots__ = ("name", "t", "last_w", "readers", "ds")

    def __init__(self, name, t):
        self.name = name
        self.t = t
        self.last_w = None
        self.readers = []
        self.ds = None

    def __getitem__(self, idx):
        return self.t[idx]


class Prog:
    def __init__(self, nc, root):
        self.nc = nc
        self.stacks = [root]
        self.scope_bufs = [[]]
        self.root = root
        self.cnt = {e: 0 for e in ENGS}
        self.sem = {e: root.enter_context(nc.semaphore("pg_" + e)) for e in ENGS if e != "sync"}
        self.seen = {e: {} for e in ENGS}
        self.free_ds = []
        self.all_ds = []
        self.uid = 0
        self.ninst = 0

    def _name(self, n):
        self.uid += 1
        return "%s_%d" % (n, self.uid)

    def sbuf(self, name, shape, dtype):
        t = self.stacks[-1].enter_context(self.nc.sbuf_tensor(self._name(name), list(shape), dtype))
        b = Buf(name, t)
        self.scope_bufs[-1].append(b)
        return b

    def psum(self, name, shape, dtype=F32):
        t = self.stacks[-1].enter_context(self.nc.psum_tensor(self._name(name), list(shape), dtype))
        b = Buf(name, t)
        self.scope_bufs[-1].append(b)
        return b

    def dram(self, name, shape, dtype, kind="Internal"):
        t = self.nc.dram_tensor(name, list(shape), dtype, kind=kind)
        b = Buf(name, t)
        self.scope_bufs[0].append(b)
        return b

    def _ds(self, b):
        if b.ds is None:
            if self.free_ds:
                b.ds = self.free_ds.pop()
            else:
                b.ds = DSem(self.root.enter_context(self.nc.semaphore(self._name("ds"))))
                self.all_ds.append(b.ds)
        return b.ds

    @contextmanager
    def scope(self):
        st = ExitStack()
        self.stacks.append(st)
        self.scope_bufs.append([])
        try:
            yield
        finally:
            self.barrier()
            for b in self.scope_bufs.pop():
                if b.ds is not None:
                    if not b.ds.sw:
                        self.free_ds.append(b.ds)
                    b.ds = None
            self.stacks.pop()
            st.close()

    def _wait(self, eng, key, sem, val):
        seen = self.seen[eng]
        if seen.get(key, 0) >= val:
            return
        seen[key] = val
        getattr(self.nc, eng).wait_ge(sem, val)

    def barrier(self):
        for e in ENGS:
            for f in ENGS:
                if f == "sync" or f == e:
                    continue
                if self.cnt[f] > 0:
                    self._wait(e, ("e", f), self.sem[f], self.cnt[f])
            for ds in self.all_ds:
                if ds.cnt > 0:
                    self._wait(e, ("d", id(ds)), ds.h, ds.cnt * 16)

    def _deps(self, eng, reads, writes):
        need = {}
        toks = []
        for b in reads:
            if b.last_w is not None:
                toks.append(b.last_w)
        for b in writes:
            if b.last_w is not None:
                toks.append(b.last_w)
            toks.extend(b.readers)
        for tok in toks:
            if tok[0] == "e":
                e = tok[1]
                if e == eng and eng == "tensor":
                    continue
                key, sem, val = ("e", e), self.sem[e], tok[2]
            else:
                ds = tok[1]
                key, sem, val = ("d", id(ds)), ds.h, ds.cnt * 16
            if key not in need or need[key][1] < val:
                need[key] = (sem, val)
        for key, (sem, val) in need.items():
            self._wait(eng, key, sem, val)

    def _mark(self, tok, reads, writes):
        for b in writes:
            b.last_w = tok
            b.readers = []
        for b in reads:
            if b in writes:
                continue
            b.readers = [r for r in b.readers if not (r[0] == tok[0] and r[1] == tok[1])]
            b.readers.append(tok)

    limit = None

    def op(self, eng, fn, reads=(), writes=()):
        if self.limit is not None and self.ninst >= self.limit:
            return
        reads, writes = list(reads), list(writes)
        self._deps(eng, reads, writes)
        self.cnt[eng] += 1
        fn(getattr(self.nc, eng)).then_inc(self.sem[eng], 1)
        self._mark(("e", eng, self.cnt[eng]), reads, writes)
        self.ninst += 1

    def dma(self, q, out, in_, reads=(), writes=(), semb=None, **kw):
        if self.limit is not None and self.ninst >= self.limit:
            return
        reads, writes = list(reads), list(writes)
        if semb is None:
            cands = [b for b in writes + reads if not isinstance(b.t, bass.DRamTensorHandle)]
            semb = cands[0] if cands else writes[0]
        self._deps(q, reads, writes)
        if q == "gpsimd" and semb.ds is None:
            semb.ds = DSem(self.root.enter_context(self.nc.semaphore(self._name("sds"))))
            semb.ds.sw = True
            self.all_ds.append(semb.ds)
        ds = self._ds(semb)
        ds.cnt += 1
        getattr(self.nc, q).dma_start(out=out, in_=in_, **kw).then_inc(ds.h, 16)
        self._mark(("d", ds, ds.cnt), reads, writes)
        self.ninst += 1

    def collective(self, kind, groups, src, dst):
        self._deps("gpsimd", [src], [dst])
        ds = self._ds(dst)
        ds.cnt += 1
        self.nc.gpsimd.collective_compute(kind, ALU.bypass, replica_groups=groups,
                                          ins=[src.t.ap()], outs=[dst.t.ap()]).then_inc(ds.h, 16)
        self._mark(("d", ds, ds.cnt), [src], [dst])

    def finish(self, bufs):
        for b in bufs:
            toks = ([b.last_w] if b.last_w is not None else []) + b.readers
            for tok in toks:
                if tok[0] == "d":
                    ds = tok[1]
                    self._wait("sync", ("d", id(ds)), ds.h, ds.cnt * 16)


class Ring:
    def __init__(self, bufs):
        self.bufs = bufs
        self.i = 0

    def next(self):
        b = self.bufs[self.i % len(self.bufs)]
        self.i += 1
        return b


def lam_init_fn(layer):
    return 0.8 - 0.6 * math.exp(-0.3 * layer)


class Builder:
    def __init__(self, exchange="none", dbg=(), layers=2, ntiles=NTW, do_peer=True, do_attn=True, pool_padd=True, phases="AB", ntilesB=None):
        self.phases = phases
        self.ntilesB = ntilesB
        self.exchange = exchange
        self.dbg = set(dbg)
        self.layers = layers
        self.ntiles = ntiles
        self.do_peer = do_peer
        self.do_attn = do_attn
        self.pool_padd = pool_padd

    def kind(self, name):
        return "ExternalOutput" if name in self.dbg else "Internal"

    def build(self):
        nc = bass.Bass("TRN2", target_bir_lowering=False)
        self.nc = nc
        root = ExitStack()
        with root:
            P = Prog(nc, root)
            self.P = P
            ein = lambda n, s, dt=F32: P.dram(n, s, dt, kind="ExternalInput")
            self.x = ein("x", [TOKW, D])
            self.cosT = ein("cosT", [128, TOKW])
            self.sinT = ein("sinT", [128, TOKW])
            self.maskb = ein("maskb", [128, 64])
            self.ident_d = ein("ident", [128, 128], BF16)
            self.rotp_d = ein("rotp", [128, 128], BF16)
            self.ones_d = ein("ones", [128, 128], BF16)
            self.w_in = ein("w_in", [2 * D, INW])
            self.w_out = ein("w_out", [2 * D, D])
            self.w_q = ein("w_q", [2 * D, D])
            self.edT = ein("edT", [2 * 32 * 128, 16 * 512])
            self.eu = ein("eu", [2 * NE, D])
            self.nmix_d = ein("nmix", [128, 32])
            self.nffn_d = ein("nffn", [128, 32])
            self.nfin_d = ein("nfin", [1, D])
            self.lamv_d = ein("lamv", [8, 64])
            self.sl_d = ein("subln", [128, 2])
            self.lng_d = ein("lng", [2, 1024])
            self.lnb_d = ein("lnb", [2, 1024])
            self.gon_d = ein("gon", [2, 1024])
            self.swT_d = ein("swT", [2 * 8 * 128, 128])
            self.sbT_d = ein("sbT", [128, 16])
            self.skT_d = ein("skT", [2 * 128, 16 * 128])
            self.y = P.dram("y", [TOK, D], F32, kind="ExternalOutput")
            self.w_in_b = [P.dram("w_in_b%d" % l, [D, INW], BF16) for l in range(2)]
            self.w_out_b = [P.dram("w_out_b%d" % l, [D, D], BF16) for l in range(2)]
            self.w_q_b = [P.dram("w_q_b%d" % l, [D, D], BF16) for l in range(2)]
            self.edT_b = [P.dram("edT_b%d" % l, [32 * 128, 16 * 512], BF16) for l in range(2)]
            self.eu_b = [P.dram("eu_b%d" % l, [NE, D], BF16) for l in range(2)]
            self.qT = P.dram("qT", [8, 128, TOKW], BF16, kind=self.kind("qT"))
            self.kT_loc = P.dram("kT_loc", [1024, TOK], BF16, kind=self.kind("kT_loc"))
            self.v_loc = P.dram("v_loc", [1024, TOK], BF16, kind=self.kind("v_loc"))
            self.kT_all = P.dram("kT_all", [2048, TOK], BF16)
            self.v_all = P.dram("v_all", [2048, TOK], BF16)
            self.gateT = P.dram("gateT", [8, 128, TOKW], BF16, kind=self.kind("gateT"))
            self.xres = P.dram("xres", [TOKW, D], F32, kind=self.kind("xres"))

            self.ident = P.sbuf("ident", [128, 128], BF16)
            self.rotp = P.sbuf("rotp", [128, 128], BF16)
            self.ones = P.sbuf("ones", [128, 128], BF16)
            self.nmix = P.sbuf("nmix", [128, 32], F32)
            self.nffn = P.sbuf("nffn", [128, 32], F32)
            self.sbT = P.sbuf("sbT", [128, 16], F32)
            self.mask_s = P.sbuf("mask_s", [128, 64], F32)
            self.sl = P.sbuf("sl", [128, 2], F32)
            self.negm = P.sbuf("negm", [128, 1], F32)
            P.op("vector", lambda e: e.memset(self.negm[:], -3e-5), writes=[self.negm])
            for sb, dr in ((self.ident, self.ident_d), (self.rotp, self.rotp_d), (self.ones, self.ones_d),
                           (self.nmix, self.nmix_d), (self.nffn, self.nffn_d), (self.sbT, self.sbT_d),
                           (self.mask_s, self.maskb), (self.sl, self.sl_d)):
                P.dma("sync", sb[:], dr.t.ap(), reads=[dr], writes=[sb])

            self.cast_weights()
            for l in range(self.layers):
                self.phase_A(l)
                if "B" in self.phases:
                    self.do_exchange()
                    self.phase_B(l)
            P.finish([self.y, self.xres, self.qT, self.kT_all, self.v_all, self.gateT])
            P.barrier()
        return nc

    def cast_weights(self):
        P = self.P

        def cast(dst, src, r0, rows, step):
            for a in range(0, rows, step):
                P.dma("gpsimd", dst.t.ap()[a:a + step, :], src.t.ap()[r0 + a:r0 + a + step, :],
                      reads=[src], writes=[dst], semb=dst)

        for l in range(self.layers):
            cast(self.w_in_b[l], self.w_in, l * D, D, 1024)
            cast(self.w_out_b[l], self.w_out, l * D, D, 2048)
            cast(self.w_q_b[l], self.w_q, l * D, D, 2048)
            if self.do_peer:
                cast(self.edT_b[l], self.edT, l * 4096, 4096, 1024)
                cast(self.eu_b[l], self.eu, l * NE, NE, 4096)

    def load_x_tile(self, l, it, xt):
        P = self.P
        src = self.x if l == 0 else self.xres
        P.dma("sync", xt[:], src.t.ap()[it * 512:(it + 1) * 512, :].rearrange("(b p) d -> p b d", p=128),
              reads=[src], writes=[xt])

    def rms_stats(self, xt, junk, ss, rstd, nb=4, width=D):
        P = self.P
        P.op("vector", lambda e: e.memset(ss[:], 0.0), writes=[ss])
        for b in range(nb):
            P.op("scalar", lambda e, b=b: e.activation(out=junk[:], in_=xt[:, b, :], func=AF.Square,
                                                        accum_out=ss[:, b:b + 1]),
                 reads=[xt, ss], writes=[junk, ss])
        P.op("vector", lambda e: e.tensor_scalar(out=rstd[:], in0=ss[:], scalar1=1.0 / width, scalar2=EPS,
                                                 op0=ALU.mult, op1=ALU.add), reads=[ss], writes=[rstd])
        P.op("scalar", lambda e: e.activation(out=rstd[:], in_=rstd[:], func=AF.Sqrt), reads=[rstd], writes=[rstd])
        P.op("vector", lambda e: e.reciprocal(out=rstd[:], in_=rstd[:]), reads=[rstd], writes=[rstd])

    def norm_to_hT(self, xt, xn, junk, ss, rstd, hT, gain, gcol0, ptr):
        P = self.P
        self.rms_stats(xt, junk, ss, rstd)
        for b in range(4):
            eng = "vector" if b % 2 == 0 else "gpsimd"
            P.op(eng, lambda e, b=b: e.tensor_scalar(out=xn[:, b, :], in0=xt[:, b, :], scalar1=rstd[:, b:b + 1],
                                                     scalar2=None, op0=ALU.mult), reads=[xt, rstd], writes=[xn])
        for b in range(4):
            for cg in range(4):
                pt = ptr.next()
                for j in range(4):
                    c = cg * 4 + j
                    P.op("tensor", lambda e, b=b, c=c, j=j, pt=pt: e.transpose(
                        out=pt[:, j, :], in_=xn[:, b, c * 128:(c + 1) * 128], identity=self.ident[:]),
                        reads=[xn, self.ident], writes=[pt])
                for j in range(4):
                    c = cg * 4 + j
                    P.op("scalar", lambda e, b=b, c=c, j=j, pt=pt: e.activation(
                        out=hT[:, c, b * 128:(b + 1) * 128], in_=pt[:, j, :], func=AF.Copy,
                        scale=gain[:, gcol0 + c:gcol0 + c + 1]), reads=[pt, gain], writes=[hT])

    def wload(self, ring, wsrc, col0, ncols=512):
        P = self.P
        wb = ring.next()
        P.dma("sync", wb[:], wsrc.t.ap()[:, col0:col0 + ncols].rearrange("(c p) n -> p c n", p=128),
              reads=[wsrc], writes=[wb])
        return wb

    def group_stats(self, src3, tmp3, s, rs, ngrp=8, width=128, sub_mean=False):
        P = self.P
        bufs = src3[0]
        v = src3[1]
        if sub_mean:
            P.op("vector", lambda e: e.tensor_reduce(out=s[:], in_=v, axis=AX.X, op=ALU.add), reads=[bufs], writes=[s])
            P.op("vector", lambda e: e.tensor_scalar(out=s[:], in0=s[:], scalar1=1.0 / width, scalar2=None,
                                                     op0=ALU.mult), reads=[s], writes=[s])
            P.op("vector", lambda e: e.tensor_tensor(out=v, in0=v, in1=s[:].unsqueeze(2).to_broadcast([128, ngrp, width]),
                                                     op=ALU.subtract), reads=[bufs, s], writes=[bufs])
        t = tmp3[1]
        P.op("vector", lambda e: e.tensor_tensor(out=t, in0=v, in1=v, op=ALU.mult), reads=[bufs], writes=[tmp3[0]])
        P.op("vector", lambda e: e.tensor_reduce(out=rs[:], in_=t, axis=AX.X, op=ALU.add), reads=[tmp3[0]], writes=[rs])
        P.op("vector", lambda e: e.tensor_scalar(out=rs[:], in0=rs[:], scalar1=1.0 / width, scalar2=EPS,
                                                 op0=ALU.mult, op1=ALU.add), reads=[rs], writes=[rs])
        P.op("scalar", lambda e: e.activation(out=rs[:], in_=rs[:], func=AF.Sqrt), reads=[rs], writes=[rs])
        P.op("vector", lambda e: e.reciprocal(out=rs[:], in_=rs[:]), reads=[rs], writes=[rs])
        P.op("vector", lambda e: e.tensor_tensor(out=v, in0=v, in1=rs[:].unsqueeze(2).to_broadcast([128, ngrp, width]),
                                                 op=ALU.mult), reads=[bufs, rs], writes=[bufs])

    def phase_A(self, l):
        P = self.P
        with P.scope():
            xt = P.sbuf("xt", [128, 4, D], F32)
            xn = P.sbuf("xn", [128, 4, D], BF16)
            junk = P.sbuf("junk", [128, D], BF16)
            ss = P.sbuf("ss", [128, 4], F32)
            rstd = P.sbuf("rstd", [128, 4], F32)
            hT = P.sbuf("hT", [128, 16, 512], BF16)
            wring = Ring([P.sbuf("wb%d" % i, [128, 16, 512], BF16) for i in range(3)])
            cosb = P.sbuf("cosb", [128, 512], F32)
            sinb = P.sbuf("sinb", [128, 512], F32)
            qpre = Ring([P.sbuf("qpre%d" % i, [128, 512], BF16) for i in range(2)])
            t1r = Ring([P.sbuf("t1_%d" % i, [128, 512], F32) for i in range(1)])
            t2r = Ring([P.sbuf("t2_%d" % i, [128, 512], F32) for i in range(1)])
            qrr = Ring([P.sbuf("qr%d" % i, [128, 512], BF16) for i in range(3)])
            vt = P.sbuf("vt", [128, 4, 1024], BF16)
            gu = P.sbuf("gu", [128, 4, 1024], F32)
            gv = P.sbuf("gv", [128, 4, 1024], F32)
            tmp = P.sbuf("tmpA", [128, 1024], F32)
            og = P.sbuf("og", [128, 1024], F32)
            vln = P.sbuf("vln", [128, 1024], BF16)
            ogb = P.sbuf("ogb", [128, 1024], BF16)
            gT = P.sbuf("gT", [128, 8, 512], BF16)
            lng = P.sbuf("lng", [128, 1024], F32)
            lnb = P.sbuf("lnb", [128, 1024], F32)
            gon = P.sbuf("gon", [128, 1024], F32)
            swT = P.sbuf("swT", [128, 8, 128], BF16)
            s8 = P.sbuf("s8", [128, 8], F32)
            r8 = P.sbuf("r8", [128, 8], F32)
            ptr = Ring([P.psum("ptr%d" % i, [128, 4, 128], BF16) for i in range(2)])
            pmm = Ring([P.psum("pmm%d" % i, [128, 512], F32) for i in range(2)])
            prot = Ring([P.psum("prot%d" % i, [128, 512], F32) for i in range(2)])
            pmix = P.psum("pmix", [128, 1024], F32)

            for sb, dr in ((lng, self.lng_d), (lnb, self.lnb_d), (gon, self.gon_d)):
                P.dma("sync", sb[:], dr.t.ap()[l:l + 1, :].to_broadcast([128, 1024]), reads=[dr], writes=[sb])
            P.dma("gpsimd", swT[:], self.swT_d.t.ap()[l * 1024:(l + 1) * 1024, :].rearrange("(g q) p -> q g p", q=128),
                  reads=[self.swT_d], writes=[swT])
            wsrc = self.w_in_b[l]

            for it in range(self.ntiles):
                self.load_x_tile(l, it, xt)
                P.dma("sync", cosb[:], self.cosT.t.ap()[:, it * 512:(it + 1) * 512], reads=[self.cosT], writes=[cosb])
                P.dma("sync", sinb[:], self.sinT.t.ap()[:, it * 512:(it + 1) * 512], reads=[self.sinT], writes=[sinb])
                wnext = self.wload(wring, wsrc, 0)
                self.norm_to_hT(xt, xn, junk, ss, rstd, hT, self.nmix, l * 16, ptr)
                for gi in range(10):
                    wb = wnext
                    if gi + 1 < 10:
                        wnext = self.wload(wring, wsrc, (gi + 1) * 512)
                    if gi < 4:
                        for j in range(4):
                            head = (gi % 2) * 4 + j
                            pq = pmm.next()
                            for c in range(16):
                                P.op("tensor", lambda e, c=c, j=j, pq=pq, wb=wb: e.matmul(
                                    out=pq[:], lhsT=wb[:, c, j * 128:(j + 1) * 128], rhs=hT[:, c, :],
                                    start=(c == 0), stop=(c == 15)), reads=[wb, hT], writes=[pq])
                            qp = qpre.next()
                            P.op("scalar", lambda e, qp=qp, pq=pq: e.activation(out=qp[:], in_=pq[:], func=AF.Copy),
                                 reads=[pq], writes=[qp])
                            pr = prot.next()
                            P.op("tensor", lambda e, pr=pr, qp=qp: e.matmul(out=pr[:], lhsT=self.rotp[:], rhs=qp[:],
                                                                          start=True, stop=True),
                                 reads=[self.rotp, qp], writes=[pr])
                            t1, t2, qr = t1r.next(), t2r.next(), qrr.next()
                            P.op("scalar", lambda e, t1=t1, pq=pq: e.activation(out=t1[:], in_=pq[:], func=AF.Copy), reads=[pq], writes=[t1])
                            P.op("vector", lambda e, t1=t1: e.tensor_tensor(out=t1[:], in0=t1[:], in1=cosb[:], op=ALU.mult),
                                 reads=[t1, cosb], writes=[t1])
                            P.op("scalar", lambda e, t2=t2, pr=pr: e.activation(out=t2[:], in_=pr[:], func=AF.Copy), reads=[pr], writes=[t2])
                            P.op("vector", lambda e, t2=t2: e.tensor_tensor(out=t2[:], in0=t2[:], in1=sinb[:], op=ALU.mult),
                                 reads=[t2, sinb], writes=[t2])
                            P.op("gpsimd", lambda e, t1=t1, t2=t2, qr=qr: e.tensor_tensor(out=qr[:], in0=t1[:], in1=t2[:], op=ALU.add),
                                 reads=[t1, t2], writes=[qr])
                            rr, itl = it // 8, it % 8
                            if gi < 2:
                                dst, dbuf = self.qT.t.ap()[head, :, it * 512:(it + 1) * 512], self.qT
                            else:
                                dst, dbuf = self.kT_all.t.ap()[rr * 1024 + head * 128:rr * 1024 + (head + 1) * 128, itl * 512:(itl + 1) * 512], self.kT_all
                            P.dma("sync", dst, qr[:], reads=[qr], writes=[dbuf])
                    else:
                        for b in range(4):
                            pv = pmm.next()
                            for c in range(16):
                                P.op("tensor", lambda e, c=c, b=b, pv=pv, wb=wb: e.matmul(
                                    out=pv[:], lhsT=hT[:, c, b * 128:(b + 1) * 128], rhs=wb[:, c, :],
                                    start=(c == 0), stop=(c == 15)), reads=[wb, hT], writes=[pv])
                            if gi < 6:
                                P.op("scalar", lambda e, b=b, pv=pv, gi=gi: e.activation(
                                    out=vt[:, b, (gi - 4) * 512:(gi - 3) * 512], in_=pv[:], func=AF.Copy), reads=[pv], writes=[vt])
                            elif gi < 8:
                                P.op("scalar", lambda e, b=b, pv=pv, gi=gi: e.activation(
                                    out=gu[:, b, (gi - 6) * 512:(gi - 5) * 512], in_=pv[:], func=AF.Gelu_apprx_tanh), reads=[pv], writes=[gu])
                            else:
                                P.op("scalar", lambda e, b=b, pv=pv, gi=gi: e.activation(
                                    out=gv[:, b, (gi - 8) * 512:(gi - 7) * 512], in_=pv[:], func=AF.Gelu_apprx_tanh), reads=[pv], writes=[gv])
                        if gi == 5:
                            for b in range(4):
                                rr, kc = it // 8, (it % 8) * 4 + b
                                dst = self.v_all.t.ap()[rr * 1024:(rr + 1) * 1024, kc * 128:(kc + 1) * 128].rearrange("(h p) d -> p h d", p=128)
                                P.dma("sync", dst, vt[:, b, :].rearrange("p (h d) -> p h d", d=128), reads=[vt], writes=[self.v_all])
                for b in range(4):
                    v3 = gv[:, b, :].rearrange("p (g c) -> p g c", c=128)
                    t3 = tmp[:, :].rearrange("p (g c) -> p g c", c=128)
                    self.group_stats((gv, v3), (tmp, t3), s8, r8, sub_mean=True)
                    P.op("vector", lambda e, b=b: e.tensor_tensor(out=gv[:, b, :], in0=gv[:, b, :], in1=lng[:], op=ALU.mult),
                         reads=[gv, lng], writes=[gv])
                    P.op("vector", lambda e, b=b: e.tensor_tensor(out=vln[:], in0=gv[:, b, :], in1=lnb[:], op=ALU.add),
                         reads=[gv, lnb], writes=[vln])
                    for g in range(8):
                        P.op("tensor", lambda e, g=g: e.matmul(out=pmix[:, g * 128:(g + 1) * 128], lhsT=swT[:, g, :],
                                                              rhs=vln[:, g * 128:(g + 1) * 128], start=True, stop=True),
                             reads=[swT, vln], writes=[pmix])
                    o3 = og[:, :].rearrange("p (g c) -> p g c", c=128)
                    P.op("scalar", lambda e: e.activation(out=og[:], in_=pmix[:], func=AF.Copy), reads=[pmix], writes=[og])
                    P.op("vector", lambda e: e.tensor_tensor(
                        out=o3, in0=o3,
                        in1=self.sbT[:, l * 8:(l + 1) * 8].unsqueeze(2).to_broadcast([128, 8, 128]), op=ALU.add),
                        reads=[og, self.sbT], writes=[og])
                    P.op("vector", lambda e, b=b: e.tensor_tensor(out=og[:], in0=og[:], in1=gu[:, b, :], op=ALU.mult),
                         reads=[og, gu], writes=[og])
                    self.group_stats((og, o3), (tmp, t3), s8, r8, sub_mean=False)
                    P.op("vector", lambda e: e.tensor_tensor(out=ogb[:], in0=og[:], in1=gon[:], op=ALU.mult),
                         reads=[og, gon], writes=[ogb])
                    for cg in range(2):
                        pt = ptr.next()
                        for j in range(4):
                            g = cg * 4 + j
                            P.op("tensor", lambda e, g=g, j=j, pt=pt: e.transpose(
                                out=pt[:, j, :], in_=ogb[:, g * 128:(g + 1) * 128], identity=self.ident[:]),
                                reads=[ogb, self.ident], writes=[pt])
                        P.op("scalar", lambda e, cg=cg, b=b, pt=pt: e.activation(
                            out=gT[:, cg * 4:(cg + 1) * 4, b * 128:(b + 1) * 128], in_=pt[:, :, :], func=AF.Copy),
                            reads=[pt], writes=[gT])
                P.dma("sync", self.gateT.t.ap()[:, :, it * 512:(it + 1) * 512].rearrange("g p t -> p g t"), gT[:],
                      reads=[gT], writes=[self.gateT])

    def do_exchange(self):
        P = self.P
        if self.exchange == "none":
            return
        if self.exchange == "ag":
            groups = [[0, 1], [2, 3], [4, 5], [6, 7]]
            P.collective("AllGather", groups, self.kT_loc, self.kT_all)
            P.collective("AllGather", groups, self.v_loc, self.v_all)
        else:
            for r in range(2):
                P.dma("sync", self.kT_all.t.ap()[r * 1024:(r + 1) * 1024, :], self.kT_loc.t.ap(),
                      reads=[self.kT_loc], writes=[self.kT_all], semb=self.kT_all)
                P.dma("sync", self.v_all.t.ap()[r * 1024:(r + 1) * 1024, :], self.v_loc.t.ap(),
                      reads=[self.v_loc], writes=[self.v_all], semb=self.v_all)

    def phase_B(self, l):
        P = self.P
        lam_init = lam_init_fn(l)
        with P.scope():
            lamt = P.sbuf("lamt", [128, 4, 64], F32)
            lprod = P.sbuf("lprod", [128, 2, 64], F32)
            lsum = P.sbuf("lsum", [128, 2], F32)
            nlam = P.sbuf("nlam", [128, 1], F32)
            slc = P.sbuf("slc", [128, 1], F32)
            skT = P.sbuf("skT", [128, 16, 128], BF16)
            nfin = P.sbuf("nfin", [128, D], F32)
            P.dma("sync", lamt[:], self.lamv_d.t.ap()[l * 4:(l + 1) * 4, :].unsqueeze(0).to_broadcast([128, 4, 64]),
                  reads=[self.lamv_d], writes=[lamt])
            P.dma("gpsimd", skT[:], self.skT_d.t.ap()[l * 128:(l + 1) * 128, :].rearrange("p (c k) -> p c k", k=128),
                  reads=[self.skT_d], writes=[skT])
            if l == self.layers - 1:
                P.dma("sync", nfin[:], self.nfin_d.t.ap().to_broadcast([128, D]), reads=[self.nfin_d], writes=[nfin])
            l4 = lamt[:, :, :].rearrange("p (a b) d -> p a b d", b=2)
            P.op("vector", lambda e: e.tensor_tensor(out=lprod[:], in0=l4[:, :, 0, :], in1=l4[:, :, 1, :], op=ALU.mult),
                 reads=[lamt], writes=[lprod])
            P.op("vector", lambda e: e.tensor_reduce(out=lsum[:], in_=lprod[:], axis=AX.X, op=ALU.add), reads=[lprod], writes=[lsum])
            P.op("scalar", lambda e: e.activation(out=lsum[:], in_=lsum[:], func=AF.Exp), reads=[lsum], writes=[lsum])
            P.op("vector", lambda e: e.tensor_tensor(out=nlam[:], in0=lsum[:, 1:2], in1=lsum[:, 0:1], op=ALU.subtract),
                 reads=[lsum], writes=[nlam])
            P.op("vector", lambda e: e.tensor_scalar(out=nlam[:], in0=nlam[:], scalar1=-lam_init, scalar2=None, op0=ALU.add),
                 reads=[nlam], writes=[nlam])
            P.op("vector", lambda e: e.tensor_scalar(out=slc[:], in0=self.sl[:, l:l + 1], scalar1=(1.0 - lam_init), scalar2=None,
                                                     op0=ALU.mult), reads=[self.sl], writes=[slc])
            self.nlam, self.slc, self.skT, self.nfin = nlam, slc, skT, nfin

            for it in range((self.ntiles if l < self.layers - 1 else min(self.ntiles, NT)) if self.ntilesB is None else self.ntilesB):
                with P.scope():
                    xt = P.sbuf("xt", [128, 4, D], F32)
                    self.load_x_tile(l, it, xt)
                    self.attn_tile(l, it, xt)
                    if self.do_peer:
                        self.peer_tile(l, it, xt)
                    if l == self.layers - 1:
                        self.final_tile(it, xt)
                    else:
                        P.dma("sync", self.xres.t.ap()[it * 512:(it + 1) * 512, :].rearrange("(b p) d -> p b d", p=128), xt[:],
                              reads=[xt], writes=[self.xres])

    def final_tile(self, it, xt):
        P = self.P
        with P.scope():
            junk = P.sbuf("junkF", [128, D], BF16)
            ss = P.sbuf("ssF", [128, 4], F32)
            rstd = P.sbuf("rstdF", [128, 4], F32)
            self.rms_stats(xt, junk, ss, rstd)
            for b in range(4):
                P.op("vector", lambda e, b=b: e.scalar_tensor_tensor(out=xt[:, b, :], in0=xt[:, b, :], scalar=rstd[:, b:b + 1],
                                                                      in1=self.nfin[:], op0=ALU.mult, op1=ALU.mult),
                     reads=[xt, rstd, self.nfin], writes=[xt])
            P.dma("sync", self.y.t.ap()[it * 512:(it + 1) * 512, :].rearrange("(b p) d -> p b d", p=128), xt[:],
                  reads=[xt], writes=[self.y])

    def attn_tile(self, l, it, xt):
        P = self.P
        with P.scope():
            qt = P.sbuf("qt", [128, 8, 512], BF16)
            gt = P.sbuf("gt", [128, 8, 512], BF16)
            at = P.sbuf("at", [128, 8, 512], BF16)
            P.dma("sync", qt[:], self.qT.t.ap()[:, :, it * 512:(it + 1) * 512].rearrange("h p t -> p h t"),
                  reads=[self.qT], writes=[qt])
            P.dma("sync", gt[:], self.gateT.t.ap()[:, :, it * 512:(it + 1) * 512].rearrange("g p t -> p g t"),
                  reads=[self.gateT], writes=[gt])
            wring = Ring([P.sbuf("wo%d" % i, [128, 16, 512], BF16) for i in range(2)])
            pmm = Ring([P.psum("pmo%d" % i, [128, 512], F32) for i in range(2)])
            otr = Ring([P.sbuf("ot%d" % i, [128, 512], F32) for i in range(2)])
            if self.do_attn:
                kring = Ring([P.sbuf("kth%d" % i, [128, 8192], BF16) for i in range(2)])
                vring = Ring([P.sbuf("vh%d" % i, [128, 64, 128], BF16) for i in range(2)])
                ptr_ = Ring([P.sbuf("pT%d" % i, [128, 512], BF16) for i in range(4)])
                qzr = Ring([P.sbuf("qz%d" % i, [128, 2, 512], BF16) for i in range(2)])
                accD = P.sbuf("accD", [128, 512], F32)
                accP = P.sbuf("accP", [128, 512], F32)
                accB = P.sbuf("accB", [128, 512], BF16)
                for qz_ in qzr.bufs:
                    P.op("gpsimd", lambda e, qz_=qz_: e.memset(qz_[:], 0.0), writes=[qz_])
                r0 = P.sbuf("r0", [128, 512], F32)
                a0 = P.sbuf("a0", [128, 512], F32)
                a1 = P.sbuf("a1", [128, 512], F32)
                sq = P.sbuf("sqb", [128, 512], BF16)
                psT = Ring([P.psum("psT%d" % i, [128, 512], F32) for i in range(2)])
                poT = [P.psum("poT%d" % i, [128, 512], F32) for i in range(2)]
                pzb = [P.psum("pzb%d" % i, [128, 512], F32) for i in range(2)]

                def load_kv(h):
                    kth, vh = kring.next(), vring.next()
                    for r in range(2):
                        P.dma("sync", kth[:, r * 4096:(r + 1) * 4096],
                              self.kT_all.t.ap()[r * 1024 + h * 128:r * 1024 + (h + 1) * 128, :],
                              reads=[self.kT_all], writes=[kth])
                        P.dma("sync", vh[:, r * 32:(r + 1) * 32, :],
                              self.v_all.t.ap()[r * 1024 + h * 128:r * 1024 + (h + 1) * 128, :].rearrange("p (k d) -> p k d", d=128),
                              reads=[self.v_all], writes=[vh])
                    return kth, vh

                nxt = load_kv(0)
                for h in range(8):
                    kth, vh = nxt
                    if h + 1 < 8:
                        nxt = load_kv(h + 1)
                    qz = qzr.next()
                    P.op("gpsimd", lambda e, qz=qz, h=h: e.tensor_copy(out=qz[0:64, 0, :], in_=qt[0:64, h, :]), reads=[qt], writes=[qz])
                    P.op("gpsimd", lambda e, qz=qz, h=h: e.tensor_copy(out=qz[64:128, 1, :], in_=qt[64:128, h, :]), reads=[qt], writes=[qz])
                    for c in range(2):
                        oT, zb = poT[c], pzb[c]

                        def S(kc):
                            ps = psT.next()
                            P.op("tensor", lambda e, kc=kc, ps=ps: e.matmul(
                                out=ps[:], lhsT=kth[:, kc * 128:(kc + 1) * 128], rhs=qz[:, c, :], start=True, stop=True),
                                reads=[kth, qz], writes=[ps])
                            return ps
                        ps_next = S(0)
                        for kc in range(64):
                            ps = ps_next
                            if kc + 1 < 64:
                                ps_next = S(kc + 1)
                            pT = ptr_.next()
                            P.op("scalar", lambda e, kc=kc, ps=ps, pT=pT: e.activation(
                                out=pT[:], in_=ps[:], func=AF.Exp, bias=self.mask_s[:, kc:kc + 1], scale=0.125),
                                reads=[ps, self.mask_s], writes=[pT])
                            P.op("tensor", lambda e, kc=kc, pT=pT: e.matmul(out=oT[:], lhsT=vh[:, kc, :], rhs=pT[:],
                                                                           start=(kc == 0), stop=(kc == 63)),
                                 reads=[vh, pT], writes=[oT])
                            aeng, acc = ("vector", accD) if kc % 2 == 0 else ("gpsimd", accP)
                            if kc < 2:
                                P.op(aeng, lambda e, pT=pT, acc=acc: e.tensor_copy(out=acc[:], in_=pT[:]), reads=[pT], writes=[acc])
                            else:
                                P.op(aeng, lambda e, pT=pT, acc=acc: e.tensor_tensor(out=acc[:], in0=acc[:], in1=pT[:], op=ALU.add),
                                     reads=[pT, acc], writes=[acc])
                        P.op("vector", lambda e: e.tensor_tensor(out=accB[:], in0=accD[:], in1=accP[:], op=ALU.add),
                             reads=[accD, accP], writes=[accB])
                        P.op("tensor", lambda e, zb=zb: e.matmul(out=zb[:], lhsT=self.ones[:], rhs=accB[:], start=True, stop=True),
                             reads=[self.ones, accB], writes=[zb])
                    P.op("scalar", lambda e: e.activation(out=r0[:], in_=pzb[0][:], func=AF.Copy), reads=[pzb[0]], writes=[r0])
                    P.op("vector", lambda e: e.reciprocal(out=r0[:], in_=r0[:]), reads=[r0], writes=[r0])
                    P.op("scalar", lambda e: e.activation(out=a0[:], in_=poT[0][:], func=AF.Copy), reads=[poT[0]], writes=[a0])
                    P.op("vector", lambda e: e.tensor_tensor(out=a0[:], in0=a0[:], in1=r0[:], op=ALU.mult),
                         reads=[a0, r0], writes=[a0])
                    P.op("scalar", lambda e: e.activation(out=r0[:], in_=pzb[1][:], func=AF.Copy), reads=[pzb[1]], writes=[r0])
                    P.op("vector", lambda e: e.reciprocal(out=r0[:], in_=r0[:]), reads=[r0], writes=[r0])
                    P.op("scalar", lambda e: e.activation(out=a1[:], in_=poT[1][:], func=AF.Copy), reads=[poT[1]], writes=[a1])
                    P.op("vector", lambda e: e.tensor_tensor(out=a1[:], in0=a1[:], in1=r0[:], op=ALU.mult),
                         reads=[a1, r0], writes=[a1])
                    P.op("vector", lambda e: e.scalar_tensor_tensor(out=a0[:], in0=a1[:], scalar=self.nlam[:, 0:1], in1=a0[:],
                                                                     op0=ALU.mult, op1=ALU.add),
                         reads=[a1, a0, self.nlam], writes=[a0])
                    P.op("vector", lambda e: e.tensor_tensor(out=sq[:], in0=a0[:], in1=a0[:], op=ALU.mult), reads=[a0], writes=[sq])
                    pss = pmm.next()
                    P.op("tensor", lambda e, pss=pss: e.matmul(out=pss[:], lhsT=self.ones[:], rhs=sq[:], start=True, stop=True),
                         reads=[self.ones, sq], writes=[pss])
                    P.op("scalar", lambda e, pss=pss: e.activation(out=a1[:], in_=pss[:], func=AF.Copy), reads=[pss], writes=[a1])
                    P.op("vector", lambda e: e.tensor_scalar(out=a1[:], in0=a1[:], scalar1=1.0 / 128, scalar2=EPS,
                                                             op0=ALU.mult, op1=ALU.add), reads=[a1], writes=[a1])
                    P.op("scalar", lambda e: e.activation(out=a1[:], in_=a1[:], func=AF.Sqrt), reads=[a1], writes=[a1])
                    P.op("vector", lambda e: e.reciprocal(out=a1[:], in_=a1[:]), reads=[a1], writes=[a1])
                    P.op("vector", lambda e, h=h: e.scalar_tensor_tensor(out=at[:, h, :], in0=a0[:], scalar=self.slc[:, 0:1], in1=a1[:],
                                                                          op0=ALU.mult, op1=ALU.mult),
                         reads=[a0, a1, self.slc], writes=[at])
            else:
                P.op("vector", lambda e: e.memset(at[:], 0.0), writes=[at])

            wsrc = self.w_out_b[l]
            wnext = self.wload(wring, wsrc, 0)
            for ng in range(4):
                wb = wnext
                if ng + 1 < 4:
                    wnext = self.wload(wring, wsrc, (ng + 1) * 512)
                for b in range(4):
                    po = pmm.next()
                    for c in range(16):
                        src = at if c < 8 else gt
                        P.op("tensor", lambda e, c=c, b=b, po=po, wb=wb, src=src: e.matmul(
                            out=po[:], lhsT=src[:, c % 8, b * 128:(b + 1) * 128], rhs=wb[:, c, :],
                            start=(c == 0), stop=(c == 15)), reads=[wb, src], writes=[po])
                    ot = otr.next()
                    P.op("scalar", lambda e, po=po, ot=ot: e.activation(out=ot[:], in_=po[:], func=AF.Copy), reads=[po], writes=[ot])
                    P.op("vector", lambda e, b=b, ng=ng, ot=ot: e.tensor_tensor(
                        out=xt[:, b, ng * 512:(ng + 1) * 512], in0=xt[:, b, ng * 512:(ng + 1) * 512], in1=ot[:], op=ALU.add),
                        reads=[xt, ot], writes=[xt])

    def peer_tile(self, l, it, xt):
        P = self.P
        IQ = 16
        NP = 128 // IQ
        with P.scope():
            hT = P.sbuf("hT2", [128, 16, 512], BF16)
            pqs = P.sbuf("pqs", [128, 16, 512], BF16)
            ptr = Ring([P.psum("ptrP%d" % i, [128, 4, 128], BF16) for i in range(2)])
            pmm = Ring([P.psum("pmP%d" % i, [128, 512], F32) for i in range(2)])
            y4 = P.psum("y4", [128, 2048], F32)
            with P.scope():
                xn = P.sbuf("xnP", [128, 4, D], BF16)
                junk = P.sbuf("junkP", [128, D], BF16)
                ss = P.sbuf("ssP", [128, 4], F32)
                rstd = P.sbuf("rstdP", [128, 4], F32)
                self.norm_to_hT(xt, xn, junk, ss, rstd, hT, self.nffn, l * 16, ptr)
                wring = Ring([P.sbuf("wq%d" % i, [128, 16, 512], BF16) for i in range(2)])
                wsrc = self.w_q_b[l]
                wnext = self.wload(wring, wsrc, 0)
                for ng in range(4):
                    wb = wnext
                    if ng + 1 < 4:
                        wnext = self.wload(wring, wsrc, (ng + 1) * 512)
                    for j in range(4):
                        pq = pmm.next()
                        for c in range(16):
                            P.op("tensor", lambda e, c=c, j=j, pq=pq, wb=wb: e.matmul(
                                out=pq[:], lhsT=wb[:, c, j * 128:(j + 1) * 128], rhs=hT[:, c, :],
                                start=(c == 0), stop=(c == 15)), reads=[wb, hT], writes=[pq])
                        P.op("scalar", lambda e, j=j, ng=ng, pq=pq: e.activation(out=pqs[:, ng * 4 + j, :], in_=pq[:], func=AF.Copy),
                             reads=[pq], writes=[pqs])
            ering = Ring([P.sbuf("edt%d" % i, [128, 16, 512], BF16) for i in range(2)])
            uring = Ring([P.sbuf("eut%d" % i, [128, 2, 2048], BF16) for i in range(2)])
            ytmp = P.sbuf("ytmp", [128, 2048], F32)
            gsr = Ring([P.sbuf("gS%d" % i, [128, IQ * 128], BF16) for i in range(3)])
            wTr = Ring([P.sbuf("wT%d" % i, [128, IQ, 128], BF16) for i in range(2)])
            G = P.sbuf("G", [128, IQ * 128], F32)
            E = Ring([P.sbuf("E%d" % i, [128, IQ, 128], F32) for i in range(2)])
            Ap = P.sbuf("Ap", [128, 8, 128], F32)
            Bp = P.sbuf("Bp", [128, 8, 128], F32)
            nb = P.sbuf("nb", [128, 8, 2], F32)
            lz = P.sbuf("lz", [128, 8], F32)
            th = P.sbuf("th", [128, 8], F32)
            sc = P.sbuf("sc", [128, 16, 128], F32)
            sv = P.sbuf("sv", [128, 8, 2, 16], F32)
            tmp1 = P.sbuf("tmp1", [128, 128], F32)
            cand = P.sbuf("cand", [128, 8, 256], F32)
            tmp2 = P.sbuf("tmp2", [128, 256], F32)
            fv = P.sbuf("fv", [128, 8, 16], F32)
            ef = P.sbuf("ef", [128, 8, 16], F32)
            nm = P.sbuf("nm", [128, 8], F32)
            zs = P.sbuf("zs", [128, 8], F32)
            rz = P.sbuf("rz", [128, 8], F32)
            edsrc = self.edT_b[l]
            eusrc = self.eu_b[l]
            sc4 = sc[:, :, :].rearrange("p (h c) k -> p h c k", c=2)

            for b in range(4):
                bs = slice(b * 128, (b + 1) * 128)
                for q4 in range(4):
                    ps = pmm.next()
                    for j in range(4):
                        ci = q4 * 4 + j
                        P.op("tensor", lambda e, ci=ci, j=j, ps=ps: e.matmul(
                            out=ps[:, j * 128:(j + 1) * 128], lhsT=pqs[:, ci, bs], rhs=self.skT[:, ci, :], start=True, stop=True),
                            reads=[pqs, self.skT], writes=[ps])
                    P.op("scalar", lambda e, q4=q4, ps=ps: e.activation(
                        out=sc[:, q4 * 4:(q4 + 1) * 4, :], in_=ps[:, :].rearrange("p (c k) -> p c k", k=128), func=AF.Copy),
                        reads=[ps], writes=[sc])
                for ci in range(16):
                    hh, cc = ci // 2, ci % 2
                    P.op("vector", lambda e, ci=ci, hh=hh, cc=cc: e.max(out=sv[:, hh, cc, 0:8], in_=sc[:, ci, :]), reads=[sc], writes=[sv])
                    P.op("vector", lambda e, ci=ci, hh=hh, cc=cc: e.match_replace(out=tmp1[:], in_to_replace=sv[:, hh, cc, 0:8],
                                                                                   in_values=sc[:, ci, :], imm_value=-1e30),
                         reads=[sc, sv], writes=[tmp1])
                    P.op("vector", lambda e, hh=hh, cc=cc: e.max(out=sv[:, hh, cc, 8:16], in_=tmp1[:]), reads=[tmp1], writes=[sv])
                c4 = cand[:, :, :].rearrange("p h (a b) -> p h a b", b=16)
                in0 = sv[:, :, 0:1, :].rearrange("p h o a -> p h a o").to_broadcast([128, 8, 16, 16])
                in1 = sv[:, :, 1:2, :].to_broadcast([128, 8, 16, 16])
                P.op("vector", lambda e: e.tensor_tensor(out=c4, in0=in0, in1=in1, op=ALU.add), reads=[sv], writes=[cand])
                for hh in range(8):
                    P.op("vector", lambda e, hh=hh: e.max(out=fv[:, hh, 0:8], in_=cand[:, hh, :]), reads=[cand], writes=[fv])
                    P.op("vector", lambda e, hh=hh: e.match_replace(out=tmp2[:], in_to_replace=fv[:, hh, 0:8],
                                                                     in_values=cand[:, hh, :], imm_value=-1e30),
                         reads=[cand, fv], writes=[tmp2])
                    P.op("vector", lambda e, hh=hh: e.max(out=fv[:, hh, 8:16], in_=tmp2[:]), reads=[tmp2], writes=[fv])
                P.op("vector", lambda e: e.tensor_scalar(out=nm[:], in0=fv[:, :, 0], scalar1=-1.0, scalar2=None, op0=ALU.mult),
                     reads=[fv], writes=[nm])
                P.op("vector", lambda e: e.tensor_tensor(out=ef[:], in0=fv[:], in1=nm[:].unsqueeze(2).to_broadcast([128, 8, 16]),
                                                         op=ALU.add), reads=[fv, nm], writes=[ef])
                P.op("scalar", lambda e: e.activation(out=ef[:], in_=ef[:], func=AF.Exp), reads=[ef], writes=[ef])
                P.op("vector", lambda e: e.tensor_reduce(out=zs[:], in_=ef[:], axis=AX.X, op=ALU.add), reads=[ef], writes=[zs])
                P.op("scalar", lambda e: e.activation(out=lz[:], in_=zs[:], func=AF.Ln), reads=[zs], writes=[lz])
                P.op("vector", lambda e: e.tensor_tensor(out=lz[:], in0=nm[:], in1=lz[:], op=ALU.subtract), reads=[nm, lz], writes=[lz])
                P.op("vector", lambda e: e.tensor_scalar(out=nb[:, :, 1], in0=sv[:, :, 1, 0], scalar1=-1.0, scalar2=None, op0=ALU.mult),
                     reads=[sv], writes=[nb])
                P.op("vector", lambda e: e.tensor_tensor(out=nb[:, :, 0], in0=lz[:], in1=sv[:, :, 1, 0], op=ALU.add),
                     reads=[sv, lz, nb], writes=[nb])
                for hh in range(8):
                    P.op("scalar", lambda e, hh=hh: e.activation(out=Ap[:, hh, :], in_=sc4[:, hh, 0, :], func=AF.Exp,
                                                                 bias=nb[:, hh, 0:1], scale=1.0), reads=[sc, nb], writes=[Ap])
                    P.op("scalar", lambda e, hh=hh: e.activation(out=Bp[:, hh, :], in_=sc4[:, hh, 1, :], func=AF.Exp,
                                                                 bias=nb[:, hh, 1:2], scale=1.0), reads=[sc, nb], writes=[Bp])
                P.op("vector", lambda e: e.tensor_tensor(out=th[:], in0=fv[:, :, 15], in1=lz[:], op=ALU.add), reads=[fv, lz], writes=[th])
                P.op("scalar", lambda e: e.activation(out=th[:], in_=th[:], func=AF.Exp, bias=self.negm[:, 0:1], scale=1.0),
                     reads=[th, self.negm], writes=[th])

                NQ = IQ * 128 // 512
                gsl = [None] * NP

                def down_group(pc, q):
                    if q == 0:
                        gsl[pc] = gsr.next()
                    gs = gsl[pc]
                    eg = pc * NQ + q
                    wb = ering.next()
                    P.dma("sync", wb[:], edsrc.t.ap()[eg * 128:(eg + 1) * 128, :].rearrange("p (c n) -> p c n", n=512),
                          reads=[edsrc], writes=[wb])
                    ps = pmm.next()
                    for c in range(16):
                        P.op("tensor", lambda e, c=c, ps=ps, wb=wb: e.matmul(
                            out=ps[:], lhsT=hT[:, c, bs], rhs=wb[:, c, :], start=(c == 0), stop=(c == 15)),
                            reads=[hT, wb], writes=[ps])
                    P.op("scalar", lambda e, q=q, ps=ps, gs=gs: e.activation(
                        out=gs[:, q * 512:(q + 1) * 512], in_=ps[:], func=AF.Gelu_apprx_tanh), reads=[ps], writes=[gs])

                def mask_head(pc, hh):
                    i0 = pc * IQ
                    ee = E.next()
                    if hh % 8 in (0, 3, 6):
                        P.op("gpsimd", lambda e, ee=ee, hh=hh, i0=i0: e.tensor_tensor(
                            out=ee[:, :, :], in0=Ap[:, hh, i0:i0 + IQ].unsqueeze(2).to_broadcast([128, IQ, 128]),
                            in1=Bp[:, hh:hh + 1, :].to_broadcast([128, IQ, 128]), op=ALU.mult), reads=[Ap, Bp], writes=[ee])
                    else:
                        for il in range(IQ):
                            P.op("scalar", lambda e, ee=ee, hh=hh, il=il, i0=i0: e.activation(
                                out=ee[:, il, :], in_=Bp[:, hh, :], func=AF.Copy, scale=Ap[:, hh, i0 + il:i0 + il + 1]),
                                reads=[Ap, Bp], writes=[ee])
                    ef2 = ee[:, :, :].rearrange("p i j -> p (i j)")
                    if hh == 0:
                        P.op("vector", lambda e, ef2=ef2, hh=hh: e.scalar_tensor_tensor(
                            out=G[:], in0=ef2, scalar=th[:, hh:hh + 1], in1=ef2, op0=ALU.is_ge, op1=ALU.mult),
                            reads=[ee, th], writes=[G])
                    else:
                        P.op("vector", lambda e, ef2=ef2, hh=hh: e.scalar_tensor_tensor(
                            out=ef2, in0=ef2, scalar=th[:, hh:hh + 1], in1=ef2, op0=ALU.is_ge, op1=ALU.mult),
                            reads=[ee, th], writes=[ee])
                        P.op("vector", lambda e, ef2=ef2: e.tensor_tensor(out=G[:], in0=G[:], in1=ef2, op=ALU.add),
                             reads=[ee, G], writes=[G])

                def wmul(pc):
                    gs = gsl[pc]
                    P.op("vector", lambda e, gs=gs: e.tensor_tensor(out=gs[:], in0=gs[:], in1=G[:], op=ALU.mult),
                         reads=[gs, G], writes=[gs])

                def transposes(pc):
                    gs = gsl[pc]
                    wT = wTr.next()
                    for cg in range(IQ // 4):
                        pt = ptr.next()
                        for j in range(4):
                            ch = cg * 4 + j
                            P.op("tensor", lambda e, ch=ch, j=j, pt=pt, gs=gs: e.transpose(
                                out=pt[:, j, :], in_=gs[:, ch * 128:(ch + 1) * 128], identity=self.ident[:]),
                                reads=[gs, self.ident], writes=[pt])
                        P.op("scalar", lambda e, cg=cg, pt=pt, wT=wT: e.activation(out=wT[:, cg * 4:(cg + 1) * 4, :], in_=pt[:, :, :], func=AF.Copy),
                             reads=[pt], writes=[wT])
                    return wT

                def up(pc, wT):
                    i0 = pc * IQ
                    for u in range(IQ // 2):
                        ut = uring.next()
                        e0 = (i0 + u * 2) * 128
                        P.dma("sync", ut[:], eusrc.t.ap()[e0:e0 + 256, :].rearrange("(c p) d -> p c d", p=128),
                              reads=[eusrc], writes=[ut])
                        for j in range(2):
                            ch = u * 2 + j
                            first = (pc == 0 and ch == 0)
                            last = (pc == NP - 1 and ch == IQ - 1)
                            for dg in range(4):
                                P.op("tensor", lambda e, ch=ch, j=j, dg=dg, ut=ut, wT=wT, first=first, last=last: e.matmul(
                                    out=y4[:, dg * 512:(dg + 1) * 512], lhsT=wT[:, ch, :], rhs=ut[:, j, dg * 512:(dg + 1) * 512],
                                    start=first, stop=last), reads=[wT, ut], writes=[y4])

                for q in range(NQ):
                    down_group(0, q)
                for q in range(NQ):
                    down_group(1, q)
                for hh in range(8):
                    mask_head(0, hh)
                wmul(0)
                for pc in range(NP):
                    wT = transposes(pc)
                    up(pc, wT)
                    for hh in range(8):
                        if pc + 1 < NP:
                            mask_head(pc + 1, hh)
                        if hh >= 8 - NQ and pc + 2 < NP:
                            down_group(pc + 2, hh - (8 - NQ))
                    if pc + 1 < NP:
                        wmul(pc + 1)
                P.op("scalar", lambda e: e.activation(out=ytmp[:], in_=y4[:], func=AF.Copy), reads=[y4], writes=[ytmp])
                P.op("vector", lambda e, b=b: e.tensor_tensor(out=xt[:, b, :], in0=xt[:, b, :], in1=ytmp[:], op=ALU.add),
                     reads=[xt, ytmp], writes=[xt])


def _rope_tables(pos0):
    pos = np.concatenate([np.arange(pos0[0], pos0[0] + TOK), np.arange(pos0[1], pos0[1] + TOK)]).astype(np.float32)
    inv = (np.float32(10000.0) ** (-np.arange(0, 64, 2, dtype=np.float32) / np.float32(64))).astype(np.float32)
    ang = (pos[:, None] * inv[None, :]).astype(np.float32)
    cos, sin = np.cos(ang).astype(np.float32), np.sin(ang).astype(np.float32)
    p = np.arange(128)
    f = p % 32
    sgn = np.where((p % 64) < 32, -1.0, 1.0).astype(np.float32)
    cosT = np.ascontiguousarray(cos[:, f].T)
    sinT = np.ascontiguousarray((sin[:, f] * sgn[None, :]).T)
    return cosT, sinT


def make_in_maps(inp, exchange="ag"):
    f32 = np.float32
    bf = ml_dtypes.bfloat16
    A = lambda a: np.ascontiguousarray(np.asarray(a, dtype=f32))
    x_all = np.concatenate([A(inp["x_prompt"]).reshape(-1, D), A(inp["x_sample"]).reshape(-1, D)], axis=0)
    p = np.arange(128)
    partner = np.where((p % 64) < 32, p + 32, p - 32)
    rotp = np.zeros((128, 128), f32)
    rotp[partner, p] = 1.0
    shared = {
        "ident": np.eye(128, dtype=f32).astype(bf),
        "rotp": rotp.astype(bf),
        "ones": np.ones((128, 128), f32).astype(bf),
        "w_in": A(inp["w_in"]).reshape(2 * D, INW),
        "w_out": A(inp["w_out"]).reshape(2 * D, D),
        "w_q": A(inp["w_query"]).reshape(2 * D, D),
        "edT": np.ascontiguousarray(A(inp["expert_down"]).reshape(2, 32, 512, 16, 128).transpose(0, 1, 4, 3, 2)).reshape(2 * 32 * 128, 16 * 512),
        "eu": A(inp["expert_up"]).reshape(2 * NE, D),
        "nmix": np.ascontiguousarray(A(inp["norm_mix"]).reshape(2, 16, 128).transpose(2, 0, 1).reshape(128, 32)),
        "nffn": np.ascontiguousarray(A(inp["norm_ffn"]).reshape(2, 16, 128).transpose(2, 0, 1).reshape(128, 32)),
        "nfin": A(inp["norm_final"]).reshape(1, D),
        "lamv": np.ascontiguousarray(np.stack([A(inp["lambda_q1"]), A(inp["lambda_k1"]), A(inp["lambda_q2"]),
                                               A(inp["lambda_k2"])], axis=1).reshape(8, 64)),
        "subln": np.ascontiguousarray(A(inp["subln"]).T),
        "lng": A(inp["gate_ln_g"]),
        "lnb": A(inp["gate_ln_b"]),
        "gon": A(inp["gate_out_norm"]),
        "swT": np.ascontiguousarray(A(inp["spatial_w"]).transpose(0, 1, 3, 2)).reshape(2 * 8 * 128, 128),
        "sbT": np.ascontiguousarray(A(inp["spatial_b"]).transpose(2, 0, 1).reshape(128, 16)),
        "skT": np.ascontiguousarray(A(inp["sub_keys"]).transpose(0, 4, 2, 1, 3)).reshape(2 * 128, 16 * 128),
    }
    tables = {}
    maps = []
    for c in range(NCORES):
        if c < 4:
            partner = c ^ 1
            pos0 = ((c % 2) * TOK, (partner % 2) * TOK)
        else:
            partner = c
            pos0 = (0, 0)
        if pos0 not in tables:
            tables[pos0] = _rope_tables(pos0)
        cosT, sinT = tables[pos0]
        mask = np.zeros((128, 64), f32)
        xc = np.concatenate([x_all[c * TOK:(c + 1) * TOK], x_all[partner * TOK:(partner + 1) * TOK]], axis=0)
        m = dict(shared)
        m.update({"x": np.ascontiguousarray(xc), "cosT": cosT, "sinT": sinT, "maskb": mask})
        maps.append(m)
    return maps


_NC_CACHE = {}


def kernel(**inputs):
    key = "full"
    if key not in _NC_CACHE:
        _NC_CACHE[key] = Builder(exchange="none").build()
    nc = _NC_CACHE[key]
    maps = make_in_maps(inputs, "none")
    res = run_bass_kernel_spmd(nc, maps, core_ids=list(range(NCORES)))
    ys = [np.asarray(r["y"], dtype=np.float32) for r in res.results]
    y_prompt = np.concatenate(ys[:4], axis=0).reshape(2, 8192, D)
    y_sample = np.concatenate(ys[4:], axis=0).reshape(4, 4096, D)
    return (y_prompt, y_sample)
```

```python
import math
from contextlib import ExitStack, contextmanager

import numpy as np
import ml_dtypes

import concourse.bass as bass
import concourse.mybir as mybir
from concourse.bass_utils import run_bass_kernel_spmd

F32 = mybir.dt.float32
BF16 = mybir.dt.bfloat16
AF = mybir.ActivationFunctionType
ALU = mybir.AluOpType
AX = mybir.AxisListType

NCORES = 8
TOK = 4096
D = 2048
NT = 8
TOKW = 8192
NTW = 16
INW = 5120
NE = 16384
EPS = 1e-6
NEG = -30000.0

ENGS = ("tensor", "vector", "scalar", "gpsimd", "sync")


class DSem:
    __slots__ = ("h", "cnt", "sw")

    def __init__(self, h):
        self.h = h
        self.cnt = 0
        self.sw = False


class Buf:
    __slots__ = ("name", "t", "last_w", "readers", "ds")

    def __init__(self, name, t):
        self.name = name
        self.t = t
        self.last_w = None
        self.readers = []
        self.ds = None

    def __getitem__(self, idx):
        return self.t[idx]


class Prog:
    def __init__(self, nc, root):
        self.nc = nc
        self.stacks = [root]
        self.scope_bufs = [[]]
        self.root = root
        self.cnt = {e: 0 for e in ENGS}
        self.sem = {e: root.enter_context(nc.semaphore("pg_" + e)) for e in ENGS if e != "sync"}
        self.seen = {e: {} for e in ENGS}
        self.free_ds = []
        self.all_ds = []
        self.uid = 0
        self.ninst = 0

    def _name(self, n):
        self.uid += 1
        return "%s_%d" % (n, self.uid)

    def sbuf(self, name, shape, dtype):
        t = self.stacks[-1].enter_context(self.nc.sbuf_tensor(self._name(name), list(shape), dtype))
        b = Buf(name, t)
        self.scope_bufs[-1].append(b)
        return b

    def psum(self, name, shape, dtype=F32):
        t = self.stacks[-1].enter_context(self.nc.psum_tensor(self._name(name), list(shape), dtype))
        b = Buf(name, t)
        self.scope_bufs[-1].append(b)
        return b

    def dram(self, name, shape, dtype, kind="Internal"):
        t = self.nc.dram_tensor(name, list(shape), dtype, kind=kind)
        b = Buf(name, t)
        self.scope_bufs[0].append(b)
        return b

    def _ds(self, b):
        if b.ds is None:
            if self.free_ds:
                b.ds = self.free_ds.pop()
            else:
                b.ds = DSem(self.root.enter_context(self.nc.semaphore(self._name("ds"))))
                self.all_ds.append(b.ds)
        return b.ds

    @contextmanager
    def scope(self):
        st = ExitStack()
        self.stacks.append(st)
        self.scope_bufs.append([])
        try:
            yield
        finally:
            self.barrier()
            for b in self.scope_bufs.pop():
                if b.ds is not None:
                    if not b.ds.sw:
                        self.free_ds.append(b.ds)
                    b.ds = None
            self.stacks.pop()
            st.close()

    def _wait(self, eng, key, sem, val):
        seen = self.seen[eng]
        if seen.get(key, 0) >= val:
            return
        seen[key] = val
        getattr(self.nc, eng).wait_ge(sem, val)

    def barrier(self):
        for e in ENGS:
            for f in ENGS:
                if f == "sync" or f == e:
                    continue
                if self.cnt[f] > 0:
                    self._wait(e, ("e", f), self.sem[f], self.cnt[f])
            for ds in self.all_ds:
                if ds.cnt > 0:
                    self._wait(e, ("d", id(ds)), ds.h, ds.cnt * 16)

    def _deps(self, eng, reads, writes):
        need = {}
        toks = []
        for b in reads:
            if b.last_w is not None:
                toks.append(b.last_w)
        for b in writes:
            if b.last_w is not None:
                toks.append(b.last_w)
            toks.extend(b.readers)
        for tok in toks:
            if tok[0] == "e":
                e = tok[1]
                if e == eng and eng == "tensor":
                    continue
                key, sem, val = ("e", e), self.sem[e], tok[2]
            else:
                ds = tok[1]
                key, sem, val = ("d", id(ds)), ds.h, ds.cnt * 16
            if key not in need or need[key][1] < val:
                need[key] = (sem, val)
        for key, (sem, val) in need.items():
            self._wait(eng, key, sem, val)

    def _mark(self, tok, reads, writes):
        for b in writes:
            b.last_w = tok
            b.readers = []
        for b in reads:
            if b in writes:
                continue
            b.readers = [r for r in b.readers if not (r[0] == tok[0] and r[1] == tok[1])]
            b.readers.append(tok)

    limit = None

    def op(self, eng, fn, reads=(), writes=()):
        if self.limit is not None and self.ninst >= self.limit:
            return
        reads, writes = list(reads), list(writes)
        self._deps(eng, reads, writes)
        self.cnt[eng] += 1
        fn(getattr(self.nc, eng)).then_inc(self.sem[eng], 1)
        self._mark(("e", eng, self.cnt[eng]), reads, writes)
        self.ninst += 1

    def dma(self, q, out, in_, reads=(), writes=(), semb=None, **kw):
        if self.limit is not None and self.ninst >= self.limit:
            return
        reads, writes = list(reads), list(writes)
        if semb is None:
            cands = [b for b in writes + reads if not isinstance(b.t, bass.DRamTensorHandle)]
            semb = cands[0] if cands else writes[0]
        self._deps(q, reads, writes)
        if q == "gpsimd" and semb.ds is None:
            semb.ds = DSem(self.root.enter_context(self.nc.semaphore(self._name("sds"))))
            semb.ds.sw = True
            self.all_ds.append(semb.ds)
        ds = self._ds(semb)
        ds.cnt += 1
        getattr(self.nc, q).dma_start(out=out, in_=in_, **kw).then_inc(ds.h, 16)
        self._mark(("d", ds, ds.cnt), reads, writes)
        self.ninst += 1

    def collective(self, kind, groups, src, dst):
        self._deps("gpsimd", [src], [dst])
        ds = self._ds(dst)
        ds.cnt += 1
        self.nc.gpsimd.collective_compute(kind, ALU.bypass, replica_groups=groups,
                                          ins=[src.t.ap()], outs=[dst.t.ap()]).then_inc(ds.h, 16)
        self._mark(("d", ds, ds.cnt), [src], [dst])

    def finish(self, bufs):
        for b in bufs:
            toks = ([b.last_w] if b.last_w is not None else []) + b.readers
            for tok in toks:
                if tok[0] == "d":
                    ds = tok[1]
                    self._wait("sync", ("d", id(ds)), ds.h, ds.cnt * 16)


class Ring:
    def __init__(self, bufs):
        self.bufs = bufs
        self.i = 0

    def next(self):
        b = self.bufs[self.i % len(self.bufs)]
        self.i += 1
        return b


def lam_init_fn(layer):
    return 0.8 - 0.6 * math.exp(-0.3 * layer)


class Builder:
    def __init__(self, exchange="none", dbg=(), layers=2, ntiles=NTW, do_peer=True, do_attn=True, pool_padd=True, phases="AB", ntilesB=None):
        self.phases = phases
        self.ntilesB = ntilesB
        self.exchange = exchange
        self.dbg = set(dbg)
        self.layers = layers
        self.ntiles = ntiles
        self.do_peer = do_peer
        self.do_attn = do_attn
        self.pool_padd = pool_padd

    def kind(self, name):
        return "ExternalOutput" if name in self.dbg else "Internal"

    def build(self):
        nc = bass.Bass("TRN2", target_bir_lowering=False)
        self.nc = nc
        root = ExitStack()
        with root:
            P = Prog(nc, root)
            self.P = P
            ein = lambda n, s, dt=F32: P.dram(n, s, dt, kind="ExternalInput")
            self.x = ein("x", [TOKW, D])
            self.cosT = ein("cosT", [128, TOKW])
            self.sinT = ein("sinT", [128, TOKW])
            self.maskb = ein("maskb", [128, 64])
            self.ident_d = ein("ident", [128, 128], BF16)
            self.rotp_d = ein("rotp", [128, 128], BF16)
            self.ones_d = ein("ones", [128, 128], BF16)
            self.w_in = ein("w_in", [2 * D, INW])
            self.w_out = ein("w_out", [2 * D, D])
            self.w_q = ein("w_q", [2 * D, D])
            self.edT = ein("edT", [2 * 32 * 128, 16 * 512])
            self.eu = ein("eu", [2 * NE, D])
            self.nmix_d = ein("nmix", [128, 32])
            self.nffn_d = ein("nffn", [128, 32])
            self.nfin_d = ein("nfin", [1, D])
            self.lamv_d = ein("lamv", [8, 64])
            self.sl_d = ein("subln", [128, 2])
            self.lng_d = ein("lng", [2, 1024])
            self.lnb_d = ein("lnb", [2, 1024])
            self.gon_d = ein("gon", [2, 1024])
            self.swT_d = ein("swT", [2 * 8 * 128, 128])
            self.sbT_d = ein("sbT", [128, 16])
            self.skT_d = ein("skT", [2 * 128, 16 * 128])
            self.y = P.dram("y", [TOK, D], F32, kind="ExternalOutput")
            self.w_in_b = [P.dram("w_in_b%d" % l, [D, INW], BF16) for l in range(2)]
            self.w_out_b = [P.dram("w_out_b%d" % l, [D, D], BF16) for l in range(2)]
            self.w_q_b = [P.dram("w_q_b%d" % l, [D, D], BF16) for l in range(2)]
            self.edT_b = [P.dram("edT_b%d" % l, [32 * 128, 16 * 512], BF16) for l in range(2)]
            self.eu_b = [P.dram("eu_b%d" % l, [NE, D], BF16) for l in range(2)]
            self.qT = P.dram("qT", [8, 128, TOKW], BF16, kind=self.kind("qT"))
            self.kT_loc = P.dram("kT_loc", [1024, TOK], BF16, kind=self.kind("kT_loc"))
            self.v_loc = P.dram("v_loc", [1024, TOK], BF16, kind=self.kind("v_loc"))
            self.kT_all = P.dram("kT_all", [2048, TOK], BF16)
            self.v_all = P.dram("v_all", [2048, TOK], BF16)
            self.gateT = P.dram("gateT", [8, 128, TOKW], BF16, kind=self.kind("gateT"))
            self.xres = P.dram("xres", [TOKW, D], F32, kind=self.kind("xres"))

            self.ident = P.sbuf("ident", [128, 128], BF16)
            self.rotp = P.sbuf("rotp", [128, 128], BF16)
            self.ones = P.sbuf("ones", [128, 128], BF16)
            self.nmix = P.sbuf("nmix", [128, 32], F32)
            self.nffn = P.sbuf("nffn", [128, 32], F32)
            self.sbT = P.sbuf("sbT", [128, 16], F32)
            self.mask_s = P.sbuf("mask_s", [128, 64], F32)
            self.sl = P.sbuf("sl", [128, 2], F32)
            self.negm = P.sbuf("negm", [128, 1], F32)
            P.op("vector", lambda e: e.memset(self.negm[:], -3e-5), writes=[self.negm])
            for sb, dr in ((self.ident, self.ident_d), (self.rotp, self.rotp_d), (self.ones, self.ones_d),
                           (self.nmix, self.nmix_d), (self.nffn, self.nffn_d), (self.sbT, self.sbT_d),
                           (self.mask_s, self.maskb), (self.sl, self.sl_d)):
                P.dma("sync", sb[:], dr.t.ap(), reads=[dr], writes=[sb])

            self.cast_weights()
            for l in range(self.layers):
                self.phase_A(l)
                if "B" in self.phases:
                    self.do_exchange()
                    self.phase_B(l)
            P.finish([self.y, self.xres, self.qT, self.kT_all, self.v_all, self.gateT])
            P.barrier()
        return nc

    def cast_weights(self):
        P = self.P

        def cast(dst, src, r0, rows, step):
            for a in range(0, rows, step):
                P.dma("gpsimd", dst.t.ap()[a:a + step, :], src.t.ap()[r0 + a:r0 + a + step, :],
                      reads=[src], writes=[dst], semb=dst)

        for l in range(self.layers):
            cast(self.w_in_b[l], self.w_in, l * D, D, 1024)
            cast(self.w_out_b[l], self.w_out, l * D, D, 2048)
            cast(self.w_q_b[l], self.w_q, l * D, D, 2048)
            if self.do_peer:
                cast(self.edT_b[l], self.edT, l * 4096, 4096, 1024)
                cast(self.eu_b[l], self.eu, l * NE, NE, 4096)

    def load_x_tile(self, l, it, xt):
        P = self.P
        src = self.x if l == 0 else self.xres
        P.dma("sync", xt[:], src.t.ap()[it * 512:(it + 1) * 512, :].rearrange("(b p) d -> p b d", p=128),
              reads=[src], writes=[xt])

    def rms_stats(self, xt, junk, ss, rstd, nb=4, width=D):
        P = self.P
        P.op("vector", lambda e: e.memset(ss[:], 0.0), writes=[ss])
        for b in range(nb):
            P.op("scalar", lambda e, b=b: e.activation(out=junk[:], in_=xt[:, b, :], func=AF.Square,
                                                        accum_out=ss[:, b:b + 1]),
                 reads=[xt, ss], writes=[junk, ss])
        P.op("vector", lambda e: e.tensor_scalar(out=rstd[:], in0=ss[:], scalar1=1.0 / width, scalar2=EPS,
                                                 op0=ALU.mult, op1=ALU.add), reads=[ss], writes=[rstd])
        P.op("scalar", lambda e: e.activation(out=rstd[:], in_=rstd[:], func=AF.Sqrt), reads=[rstd], writes=[rstd])
        P.op("vector", lambda e: e.reciprocal(out=rstd[:], in_=rstd[:]), reads=[rstd], writes=[rstd])

    def norm_to_hT(self, xt, xn, junk, ss, rstd, hT, gain, gcol0, ptr):
        P = self.P
        self.rms_stats(xt, junk, ss, rstd)
        for b in range(4):
            eng = "vector" if b % 2 == 0 else "gpsimd"
            P.op(eng, lambda e, b=b: e.tensor_scalar(out=xn[:, b, :], in0=xt[:, b, :], scalar1=rstd[:, b:b + 1],
                                                     scalar2=None, op0=ALU.mult), reads=[xt, rstd], writes=[xn])
        for b in range(4):
            for cg in range(4):
                pt = ptr.next()
                for j in range(4):
                    c = cg * 4 + j
                    P.op("tensor", lambda e, b=b, c=c, j=j, pt=pt: e.transpose(
                        out=pt[:, j, :], in_=xn[:, b, c * 128:(c + 1) * 128], identity=self.ident[:]),
                        reads=[xn, self.ident], writes=[pt])
                for j in range(4):
                    c = cg * 4 + j
                    P.op("scalar", lambda e, b=b, c=c, j=j, pt=pt: e.activation(
                        out=hT[:, c, b * 128:(b + 1) * 128], in_=pt[:, j, :], func=AF.Copy,
                        scale=gain[:, gcol0 + c:gcol0 + c + 1]), reads=[pt, gain], writes=[hT])

    def wload(self, ring, wsrc, col0, ncols=512):
        P = self.P
        wb = ring.next()
        P.dma("sync", wb[:], wsrc.t.ap()[:, col0:col0 + ncols].rearrange("(c p) n -> p c n", p=128),
              reads=[wsrc], writes=[wb])
        return wb

    def group_stats(self, src3, tmp3, s, rs, ngrp=8, width=128, sub_mean=False):
        P = self.P
        bufs = src3[0]
        v = src3[1]
        if sub_mean:
            P.op("vector", lambda e: e.tensor_reduce(out=s[:], in_=v, axis=AX.X, op=ALU.add), reads=[bufs], writes=[s])
            P.op("vector", lambda e: e.tensor_scalar(out=s[:], in0=s[:], scalar1=1.0 / width, scalar2=None,
                                                     op0=ALU.mult), reads=[s], writes=[s])
            P.op("vector", lambda e: e.tensor_tensor(out=v, in0=v, in1=s[:].unsqueeze(2).to_broadcast([128, ngrp, width]),
                                                     op=ALU.subtract), reads=[bufs, s], writes=[bufs])
        t = tmp3[1]
        P.op("vector", lambda e: e.tensor_tensor(out=t, in0=v, in1=v, op=ALU.mult), reads=[bufs], writes=[tmp3[0]])
        P.op("vector", lambda e: e.tensor_reduce(out=rs[:], in_=t, axis=AX.X, op=ALU.add), reads=[tmp3[0]], writes=[rs])
        P.op("vector", lambda e: e.tensor_scalar(out=rs[:], in0=rs[:], scalar1=1.0 / width, scalar2=EPS,
                                                 op0=ALU.mult, op1=ALU.add), reads=[rs], writes=[rs])
        P.op("scalar", lambda e: e.activation(out=rs[:], in_=rs[:], func=AF.Sqrt), reads=[rs], writes=[rs])
        P.op("vector", lambda e: e.reciprocal(out=rs[:], in_=rs[:]), reads=[rs], writes=[rs])
        P.op("vector", lambda e: e.tensor_tensor(out=v, in0=v, in1=rs[:].unsqueeze(2).to_broadcast([128, ngrp, width]),
                                                 op=ALU.mult), reads=[bufs, rs], writes=[bufs])

    def phase_A(self, l):
        P = self.P
        with P.scope():
            xt = P.sbuf("xt", [128, 4, D], F32)
            xn = P.sbuf("xn", [128, 4, D], BF16)
            junk = P.sbuf("junk", [128, D], BF16)
            ss = P.sbuf("ss", [128, 4], F32)
            rstd = P.sbuf("rstd", [128, 4], F32)
            hT = P.sbuf("hT", [128, 16, 512], BF16)
            wring = Ring([P.sbuf("wb%d" % i, [128, 16, 512], BF16) for i in range(3)])
            cosb = P.sbuf("cosb", [128, 512], F32)
            sinb = P.sbuf("sinb", [128, 512], F32)
            qpre = Ring([P.sbuf("qpre%d" % i, [128, 512], BF16) for i in range(2)])
            t1r = Ring([P.sbuf("t1_%d" % i, [128, 512], F32) for i in range(1)])
            t2r = Ring([P.sbuf("t2_%d" % i, [128, 512], F32) for i in range(1)])
            qrr = Ring([P.sbuf("qr%d" % i, [128, 512], BF16) for i in range(3)])
            vt = P.sbuf("vt", [128, 4, 1024], BF16)
            gu = P.sbuf("gu", [128, 4, 1024], F32)
            gv = P.sbuf("gv", [128, 4, 1024], F32)
            tmp = P.sbuf("tmpA", [128, 1024], F32)
            og = P.sbuf("og", [128, 1024], F32)
            vln = P.sbuf("vln", [128, 1024], BF16)
            ogb = P.sbuf("ogb", [128, 1024], BF16)
            gT = P.sbuf("gT", [128, 8, 512], BF16)
            lng = P.sbuf("lng", [128, 1024], F32)
            lnb = P.sbuf("lnb", [128, 1024], F32)
            gon = P.sbuf("gon", [128, 1024], F32)
            swT = P.sbuf("swT", [128, 8, 128], BF16)
            s8 = P.sbuf("s8", [128, 8], F32)
            r8 = P.sbuf("r8", [128, 8], F32)
            ptr = Ring([P.psum("ptr%d" % i, [128, 4, 128], BF16) for i in range(2)])
            pmm = Ring([P.psum("pmm%d" % i, [128, 512], F32) for i in range(2)])
            prot = Ring([P.psum("prot%d" % i, [128, 512], F32) for i in range(2)])
            pmix = P.psum("pmix", [128, 1024], F32)

            for sb, dr in ((lng, self.lng_d), (lnb, self.lnb_d), (gon, self.gon_d)):
                P.dma("sync", sb[:], dr.t.ap()[l:l + 1, :].to_broadcast([128, 1024]), reads=[dr], writes=[sb])
            P.dma("gpsimd", swT[:], self.swT_d.t.ap()[l * 1024:(l + 1) * 1024, :].rearrange("(g q) p -> q g p", q=128),
                  reads=[self.swT_d], writes=[swT])
            wsrc = self.w_in_b[l]

            for it in range(self.ntiles):
                self.load_x_tile(l, it, xt)
                P.dma("sync", cosb[:], self.cosT.t.ap()[:, it * 512:(it + 1) * 512], reads=[self.cosT], writes=[cosb])
                P.dma("sync", sinb[:], self.sinT.t.ap()[:, it * 512:(it + 1) * 512], reads=[self.sinT], writes=[sinb])
                wnext = self.wload(wring, wsrc, 0)
                self.norm_to_hT(xt, xn, junk, ss, rstd, hT, self.nmix, l * 16, ptr)
                for gi in range(10):
                    wb = wnext
                    if gi + 1 < 10:
                        wnext = self.wload(wring, wsrc, (gi + 1) * 512)
                    if gi < 4:
                        for j in range(4):
                            head = (gi % 2) * 4 + j
                            pq = pmm.next()
                            for c in range(16):
                                P.op("tensor", lambda e, c=c, j=j, pq=pq, wb=wb: e.matmul(
                                    out=pq[:], lhsT=wb[:, c, j * 128:(j + 1) * 128], rhs=hT[:, c, :],
                                    start=(c == 0), stop=(c == 15)), reads=[wb, hT], writes=[pq])
                            qp = qpre.next()
                            P.op("scalar", lambda e, qp=qp, pq=pq: e.activation(out=qp[:], in_=pq[:], func=AF.Copy),
                                 reads=[pq], writes=[qp])
                            pr = prot.next()
                            P.op("tensor", lambda e, pr=pr, qp=qp: e.matmul(out=pr[:], lhsT=self.rotp[:], rhs=qp[:],
                                                                          start=True, stop=True),
                                 reads=[self.rotp, qp], writes=[pr])
                            t1, t2, qr = t1r.next(), t2r.next(), qrr.next()
                            P.op("scalar", lambda e, t1=t1, pq=pq: e.activation(out=t1[:], in_=pq[:], func=AF.Copy), reads=[pq], writes=[t1])
                            P.op("vector", lambda e, t1=t1: e.tensor_tensor(out=t1[:], in0=t1[:], in1=cosb[:], op=ALU.mult),
                                 reads=[t1, cosb], writes=[t1])
                            P.op("scalar", lambda e, t2=t2, pr=pr: e.activation(out=t2[:], in_=pr[:], func=AF.Copy), reads=[pr], writes=[t2])
                            P.op("vector", lambda e, t2=t2: e.tensor_tensor(out=t2[:], in0=t2[:], in1=sinb[:], op=ALU.mult),
                                 reads=[t2, sinb], writes=[t2])
                            P.op("gpsimd", lambda e, t1=t1, t2=t2, qr=qr: e.tensor_tensor(out=qr[:], in0=t1[:], in1=t2[:], op=ALU.add),
                                 reads=[t1, t2], writes=[qr])
                            rr, itl = it // 8, it % 8
                            if gi < 2:
                                dst, dbuf = self.qT.t.ap()[head, :, it * 512:(it + 1) * 512], self.qT
                            else:
                                dst, dbuf = self.kT_all.t.ap()[rr * 1024 + head * 128:rr * 1024 + (head + 1) * 128, itl * 512:(itl + 1) * 512], self.kT_all
                            P.dma("sync", dst, qr[:], reads=[qr], writes=[dbuf])
                    else:
                        for b in range(4):
                            pv = pmm.next()
                            for c in range(16):
                                P.op("tensor", lambda e, c=c, b=b, pv=pv, wb=wb: e.matmul(
                                    out=pv[:], lhsT=hT[:, c, b * 128:(b + 1) * 128], rhs=wb[:, c, :],
                                    start=(c == 0), stop=(c == 15)), reads=[wb, hT], writes=[pv])
                            if gi < 6:
                                P.op("scalar", lambda e, b=b, pv=pv, gi=gi: e.activation(
                                    out=vt[:, b, (gi - 4) * 512:(gi - 3) * 512], in_=pv[:], func=AF.Copy), reads=[pv], writes=[vt])
                            elif gi < 8:
                                P.op("scalar", lambda e, b=b, pv=pv, gi=gi: e.activation(
                                    out=gu[:, b, (gi - 6) * 512:(gi - 5) * 512], in_=pv[:], func=AF.Gelu_apprx_tanh), reads=[pv], writes=[gu])
                            else:
                                P.op("scalar", lambda e, b=b, pv=pv, gi=gi: e.activation(
                                    out=gv[:, b, (gi - 8) * 512:(gi - 7) * 512], in_=pv[:], func=AF.Gelu_apprx_tanh), reads=[pv], writes=[gv])
                        if gi == 5:
                            for b in range(4):
                                rr, kc = it // 8, (it % 8) * 4 + b
                                dst = self.v_all.t.ap()[rr * 1024:(rr + 1) * 1024, kc * 128:(kc + 1) * 128].rearrange("(h p) d -> p h d", p=128)
                                P.dma("sync", dst, vt[:, b, :].rearrange("p (h d) -> p h d", d=128), reads=[vt], writes=[self.v_all])
                for b in range(4):
                    v3 = gv[:, b, :].rearrange("p (g c) -> p g c", c=128)
                    t3 = tmp[:, :].rearrange("p (g c) -> p g c", c=128)
                    self.group_stats((gv, v3), (tmp, t3), s8, r8, sub_mean=True)
                    P.op("vector", lambda e, b=b: e.tensor_tensor(out=gv[:, b, :], in0=gv[:, b, :], in1=lng[:], op=ALU.mult),
                         reads=[gv, lng], writes=[gv])
                    P.op("vector", lambda e, b=b: e.tensor_tensor(out=vln[:], in0=gv[:, b, :], in1=lnb[:], op=ALU.add),
                         reads=[gv, lnb], writes=[vln])
                    for g in range(8):
                        P.op("tensor", lambda e, g=g: e.matmul(out=pmix[:, g * 128:(g + 1) * 128], lhsT=swT[:, g, :],
                                                              rhs=vln[:, g * 128:(g + 1) * 128], start=True, stop=True),
                             reads=[swT, vln], writes=[pmix])
                    o3 = og[:, :].rearrange("p (g c) -> p g c", c=128)
                    P.op("scalar", lambda e: e.activation(out=og[:], in_=pmix[:], func=AF.Copy), reads=[pmix], writes=[og])
                    P.op("vector", lambda e: e.tensor_tensor(
                        out=o3, in0=o3,
                        in1=self.sbT[:, l * 8:(l + 1) * 8].unsqueeze(2).to_broadcast([128, 8, 128]), op=ALU.add),
                        reads=[og, self.sbT], writes=[og])
                    P.op("vector", lambda e, b=b: e.tensor_tensor(out=og[:], in0=og[:], in1=gu[:, b, :], op=ALU.mult),
                         reads=[og, gu], writes=[og])
                    self.group_stats((og, o3), (tmp, t3), s8, r8, sub_mean=False)
                    P.op("vector", lambda e: e.tensor_tensor(out=ogb[:], in0=og[:], in1=gon[:], op=ALU.mult),
                         reads=[og, gon], writes=[ogb])
                    for cg in range(2):
                        pt = ptr.next()
                        for j in range(4):
                            g = cg * 4 + j
                            P.op("tensor", lambda e, g=g, j=j, pt=pt: e.transpose(
                                out=pt[:, j, :], in_=ogb[:, g * 128:(g + 1) * 128], identity=self.ident[:]),
                                reads=[ogb, self.ident], writes=[pt])
                        P.op("scalar", lambda e, cg=cg, b=b, pt=pt: e.activation(
                            out=gT[:, cg * 4:(cg + 1) * 4, b * 128:(b + 1) * 128], in_=pt[:, :, :], func=AF.Copy),
                            reads=[pt], writes=[gT])
                P.dma("sync", self.gateT.t.ap()[:, :, it * 512:(it + 1) * 512].rearrange("g p t -> p g t"), gT[:],
                      reads=[gT], writes=[self.gateT])

    def do_exchange(self):
        P = self.P
        if self.exchange == "none":
            return
        if self.exchange == "ag":
            groups = [[0, 1], [2, 3], [4, 5], [6, 7]]
            P.collective("AllGather", groups, self.kT_loc, self.kT_all)
            P.collective("AllGather", groups, self.v_loc, self.v_all)
        else:
            for r in range(2):
                P.dma("sync", self.kT_all.t.ap()[r * 1024:(r + 1) * 1024, :], self.kT_loc.t.ap(),
                      reads=[self.kT_loc], writes=[self.kT_all], semb=self.kT_all)
                P.dma("sync", self.v_all.t.ap()[r * 1024:(r + 1) * 1024, :], self.v_loc.t.ap(),
                      reads=[self.v_loc], writes=[self.v_all], semb=self.v_all)

    def phase_B(self, l):
        P = self.P
        lam_init = lam_init_fn(l)
        with P.scope():
            lamt = P.sbuf("lamt", [128, 4, 64], F32)
            lprod = P.sbuf("lprod", [128, 2, 64], F32)
            lsum = P.sbuf("lsum", [128, 2], F32)
            nlam = P.sbuf("nlam", [128, 1], F32)
            slc = P.sbuf("slc", [128, 1], F32)
            skT = P.sbuf("skT", [128, 16, 128], BF16)
            nfin = P.sbuf("nfin", [128, D], F32)
            P.dma("sync", lamt[:], self.lamv_d.t.ap()[l * 4:(l + 1) * 4, :].unsqueeze(0).to_broadcast([128, 4, 64]),
                  reads=[self.lamv_d], writes=[lamt])
            P.dma("gpsimd", skT[:], self.skT_d.t.ap()[l * 128:(l + 1) * 128, :].rearrange("p (c k) -> p c k", k=128),
                  reads=[self.skT_d], writes=[skT])
            if l == self.layers - 1:
                P.dma("sync", nfin[:], self.nfin_d.t.ap().to_broadcast([128, D]), reads=[self.nfin_d], writes=[nfin])
            l4 = lamt[:, :, :].rearrange("p (a b) d -> p a b d", b=2)
            P.op("vector", lambda e: e.tensor_tensor(out=lprod[:], in0=l4[:, :, 0, :], in1=l4[:, :, 1, :], op=ALU.mult),
                 reads=[lamt], writes=[lprod])
            P.op("vector", lambda e: e.tensor_reduce(out=lsum[:], in_=lprod[:], axis=AX.X, op=ALU.add), reads=[lprod], writes=[lsum])
            P.op("scalar", lambda e: e.activation(out=lsum[:], in_=lsum[:], func=AF.Exp), reads=[lsum], writes=[lsum])
            P.op("vector", lambda e: e.tensor_tensor(out=nlam[:], in0=lsum[:, 1:2], in1=lsum[:, 0:1], op=ALU.subtract),
                 reads=[lsum], writes=[nlam])
            P.op("vector", lambda e: e.tensor_scalar(out=nlam[:], in0=nlam[:], scalar1=-lam_init, scalar2=None, op0=ALU.add),
                 reads=[nlam], writes=[nlam])
            P.op("vector", lambda e: e.tensor_scalar(out=slc[:], in0=self.sl[:, l:l + 1], scalar1=(1.0 - lam_init), scalar2=None,
                                                     op0=ALU.mult), reads=[self.sl], writes=[slc])
            self.nlam, self.slc, self.skT, self.nfin = nlam, slc, skT, nfin

            for it in range((self.ntiles if l < self.layers - 1 else min(self.ntiles, NT)) if self.ntilesB is None else self.ntilesB):
                with P.scope():
                    xt = P.sbuf("xt", [128, 4, D], F32)
                    self.load_x_tile(l, it, xt)
                    self.attn_tile(l, it, xt)
                    if self.do_peer:
                        self.peer_tile(l, it, xt)
                    if l == self.layers - 1:
                        self.final_tile(it, xt)
                    else:
                        P.dma("sync", self.xres.t.ap()[it * 512:(it + 1) * 512, :].rearrange("(b p) d -> p b d", p=128), xt[:],
                              reads=[xt], writes=[self.xres])

    def final_tile(self, it, xt):
        P = self.P
        with P.scope():
            junk = P.sbuf("junkF", [128, D], BF16)
            ss = P.sbuf("ssF", [128, 4], F32)
            rstd = P.sbuf("rstdF", [128, 4], F32)
            self.rms_stats(xt, junk, ss, rstd)
            for b in range(4):
                P.op("vector", lambda e, b=b: e.scalar_tensor_tensor(out=xt[:, b, :], in0=xt[:, b, :], scalar=rstd[:, b:b + 1],
                                                                      in1=self.nfin[:], op0=ALU.mult, op1=ALU.mult),
                     reads=[xt, rstd, self.nfin], writes=[xt])
            P.dma("sync", self.y.t.ap()[it * 512:(it + 1) * 512, :].rearrange("(b p) d -> p b d", p=128), xt[:],
                  reads=[xt], writes=[self.y])

    def attn_tile(self, l, it, xt):
        P = self.P
        with P.scope():
            qt = P.sbuf("qt", [128, 8, 512], BF16)
            gt = P.sbuf("gt", [128, 8, 512], BF16)
            at = P.sbuf("at", [128, 8, 512], BF16)
            P.dma("sync", qt[:], self.qT.t.ap()[:, :, it * 512:(it + 1) * 512].rearrange("h p t -> p h t"),
                  reads=[self.qT], writes=[qt])
            P.dma("sync", gt[:], self.gateT.t.ap()[:, :, it * 512:(it + 1) * 512].rearrange("g p t -> p g t"),
                  reads=[self.gateT], writes=[gt])
            wring = Ring([P.sbuf("wo%d" % i, [128, 16, 512], BF16) for i in range(2)])
            pmm = Ring([P.psum("pmo%d" % i, [128, 512], F32) for i in range(2)])
            otr = Ring([P.sbuf("ot%d" % i, [128, 512], F32) for i in range(2)])
            if self.do_attn:
                kring = Ring([P.sbuf("kth%d" % i, [128, 8192], BF16) for i in range(2)])
                vring = Ring([P.sbuf("vh%d" % i, [128, 64, 128], BF16) for i in range(2)])
                ptr_ = Ring([P.sbuf("pT%d" % i, [128, 512], BF16) for i in range(4)])
                qzr = Ring([P.sbuf("qz%d" % i, [128, 2, 512], BF16) for i in range(2)])
                accD = P.sbuf("accD", [128, 512], F32)
                accP = P.sbuf("accP", [128, 512], F32)
                accB = P.sbuf("accB", [128, 512], BF16)
                for qz_ in qzr.bufs:
                    P.op("gpsimd", lambda e, qz_=qz_: e.memset(qz_[:], 0.0), writes=[qz_])
                r0 = P.sbuf("r0", [128, 512], F32)
                a0 = P.sbuf("a0", [128, 512], F32)
                a1 = P.sbuf("a1", [128, 512], F32)
                sq = P.sbuf("sqb", [128, 512], BF16)
                psT = Ring([P.psum("psT%d" % i, [128, 512], F32) for i in range(2)])
                poT = [P.psum("poT%d" % i, [128, 512], F32) for i in range(2)]
                pzb = [P.psum("pzb%d" % i, [128, 512], F32) for i in range(2)]

                def load_kv(h):
                    kth, vh = kring.next(), vring.next()
                    for r in range(2):
                        P.dma("sync", kth[:, r * 4096:(r + 1) * 4096],
                              self.kT_all.t.ap()[r * 1024 + h * 128:r * 1024 + (h + 1) * 128, :],
                              reads=[self.kT_all], writes=[kth])
                        P.dma("sync", vh[:, r * 32:(r + 1) * 32, :],
                              self.v_all.t.ap()[r * 1024 + h * 128:r * 1024 + (h + 1) * 128, :].rearrange("p (k d) -> p k d", d=128),
                              reads=[self.v_all], writes=[vh])
                    return kth, vh

                nxt = load_kv(0)
                for h in range(8):
                    kth, vh = nxt
                    if h + 1 < 8:
                        nxt = load_kv(h + 1)
                    qz = qzr.next()
                    P.op("gpsimd", lambda e, qz=qz, h=h: e.tensor_copy(out=qz[0:64, 0, :], in_=qt[0:64, h, :]), reads=[qt], writes=[qz])
                    P.op("gpsimd", lambda e, qz=qz, h=h: e.tensor_copy(out=qz[64:128, 1, :], in_=qt[64:128, h, :]), reads=[qt], writes=[qz])
                    for c in range(2):
                        oT, zb = poT[c], pzb[c]

                        def S(kc):
                            ps = psT.next()
                            P.op("tensor", lambda e, kc=kc, ps=ps: e.matmul(
                                out=ps[:], lhsT=kth[:, kc * 128:(kc + 1) * 128], rhs=qz[:, c, :], start=True, stop=True),
                                reads=[kth, qz], writes=[ps])
                            return ps
                        ps_next = S(0)
                        for kc in range(64):
                            ps = ps_next
                            if kc + 1 < 64:
                                ps_next = S(kc + 1)
                            pT = ptr_.next()
                            P.op("scalar", lambda e, kc=kc, ps=ps, pT=pT: e.activation(
                                out=pT[:], in_=ps[:], func=AF.Exp, bias=self.mask_s[:, kc:kc + 1], scale=0.125),
                                reads=[ps, self.mask_s], writes=[pT])
                            P.op("tensor", lambda e, kc=kc, pT=pT: e.matmul(out=oT[:], lhsT=vh[:, kc, :], rhs=pT[:],
                                                                           start=(kc == 0), stop=(kc == 63)),
                                 reads=[vh, pT], writes=[oT])
                            aeng, acc = ("vector", accD) if kc % 2 == 0 else ("gpsimd", accP)
                            if kc < 2:
                                P.op(aeng, lambda e, pT=pT, acc=acc: e.tensor_copy(out=acc[:], in_=pT[:]), reads=[pT], writes=[acc])
                            else:
                                P.op(aeng, lambda e, pT=pT, acc=acc: e.tensor_tensor(out=acc[:], in0=acc[:], in1=pT[:], op=ALU.add),
                                     reads=[pT, acc], writes=[acc])
                        P.op("vector", lambda e: e.tensor_tensor(out=accB[:], in0=accD[:], in1=accP[:], op=ALU.add),
                             reads=[accD, accP], writes=[accB])
                        P.op("tensor", lambda e, zb=zb: e.matmul(out=zb[:], lhsT=self.ones[:], rhs=accB[:], start=True, stop=True),
                             reads=[self.ones, accB], writes=[zb])
                    P.op("scalar", lambda e: e.activation(out=r0[:], in_=pzb[0][:], func=AF.Copy), reads=[pzb[0]], writes=[r0])
                    P.op("vector", lambda e: e.reciprocal(out=r0[:], in_=r0[:]), reads=[r0], writes=[r0])
                    P.op("scalar", lambda e: e.activation(out=a0[:], in_=poT[0][:], func=AF.Copy), reads=[poT[0]], writes=[a0])
                    P.op("vector", lambda e: e.tensor_tensor(out=a0[:], in0=a0[:], in1=r0[:], op=ALU.mult),
                         reads=[a0, r0], writes=[a0])
                    P.op("scalar", lambda e: e.activation(out=r0[:], in_=pzb[1][:], func=AF.Copy), reads=[pzb[1]], writes=[r0])
                    P.op("vector", lambda e: e.reciprocal(out=r0[:], in_=r0[:]), reads=[r0], writes=[r0])
                    P.op("scalar", lambda e: e.activation(out=a1[:], in_=poT[1][:], func=AF.Copy), reads=[poT[1]], writes=[a1])
                    P.op("vector", lambda e: e.tensor_tensor(out=a1[:], in0=a1[:], in1=r0[:], op=ALU.mult),
                         reads=[a1, r0], writes=[a1])
                    P.op("vector", lambda e: e.scalar_tensor_tensor(out=a0[:], in0=a1[:], scalar=self.nlam[:, 0:1], in1=a0[:],
                                                                     op0=ALU.mult, op1=ALU.add),
                         reads=[a1, a0, self.nlam], writes=[a0])
                    P.op("vector", lambda e: e.tensor_tensor(out=sq[:], in0=a0[:], in1=a0[:], op=ALU.mult), reads=[a0], writes=[sq])
                    pss = pmm.next()
                    P.op("tensor", lambda e, pss=pss: e.matmul(out=pss[:], lhsT=self.ones[:], rhs=sq[:], start=True, stop=True),
                         reads=[self.ones, sq], writes=[pss])
                    P.op("scalar", lambda e, pss=pss: e.activation(out=a1[:], in_=pss[:], func=AF.Copy), reads=[pss], writes=[a1])
                    P.op("vector", lambda e: e.tensor_scalar(out=a1[:], in0=a1[:], scalar1=1.0 / 128, scalar2=EPS,
                                                             op0=ALU.mult, op1=ALU.add), reads=[a1], writes=[a1])
                    P.op("scalar", lambda e: e.activation(out=a1[:], in_=a1[:], func=AF.Sqrt), reads=[a1], writes=[a1])
                    P.op("vector", lambda e: e.reciprocal(out=a1[:], in_=a1[:]), reads=[a1], writes=[a1])
                    P.op("vector", lambda e, h=h: e.scalar_tensor_tensor(out=at[:, h, :], in0=a0[:], scalar=self.slc[:, 0:1], in1=a1[:],
                                                                          op0=ALU.mult, op1=ALU.mult),
                         reads=[a0, a1, self.slc], writes=[at])
            else:
                P.op("vector", lambda e: e.memset(at[:], 0.0), writes=[at])

            wsrc = self.w_out_b[l]
            wnext = self.wload(wring, wsrc, 0)
            for ng in range(4):
                wb = wnext
                if ng + 1 < 4:
                    wnext = self.wload(wring, wsrc, (ng + 1) * 512)
                for b in range(4):
                    po = pmm.next()
                    for c in range(16):
                        src = at if c < 8 else gt
                        P.op("tensor", lambda e, c=c, b=b, po=po, wb=wb, src=src: e.matmul(
                            out=po[:], lhsT=src[:, c % 8, b * 128:(b + 1) * 128], rhs=wb[:, c, :],
                            start=(c == 0), stop=(c == 15)), reads=[wb, src], writes=[po])
                    ot = otr.next()
                    P.op("scalar", lambda e, po=po, ot=ot: e.activation(out=ot[:], in_=po[:], func=AF.Copy), reads=[po], writes=[ot])
                    P.op("vector", lambda e, b=b, ng=ng, ot=ot: e.tensor_tensor(
                        out=xt[:, b, ng * 512:(ng + 1) * 512], in0=xt[:, b, ng * 512:(ng + 1) * 512], in1=ot[:], op=ALU.add),
                        reads=[xt, ot], writes=[xt])

    def peer_tile(self, l, it, xt):
        P = self.P
        IQ = 16
        NP = 128 // IQ
        with P.scope():
            hT = P.sbuf("hT2", [128, 16, 512], BF16)
            pqs = P.sbuf("pqs", [128, 16, 512], BF16)
            ptr = Ring([P.psum("ptrP%d" % i, [128, 4, 128], BF16) for i in range(2)])
            pmm = Ring([P.psum("pmP%d" % i, [128, 512], F32) for i in range(2)])
            y4 = P.psum("y4", [128, 2048], F32)
            with P.scope():
                xn = P.sbuf("xnP", [128, 4, D], BF16)
                junk = P.sbuf("junkP", [128, D], BF16)
                ss = P.sbuf("ssP", [128, 4], F32)
                rstd = P.sbuf("rstdP", [128, 4], F32)
                self.norm_to_hT(xt, xn, junk, ss, rstd, hT, self.nffn, l * 16, ptr)
                wring = Ring([P.sbuf("wq%d" % i, [128, 16, 512], BF16) for i in range(2)])
                wsrc = self.w_q_b[l]
                wnext = self.wload(wring, wsrc, 0)
                for ng in range(4):
                    wb = wnext
                    if ng + 1 < 4:
                        wnext = self.wload(wring, wsrc, (ng + 1) * 512)
                    for j in range(4):
                        pq = pmm.next()
                        for c in range(16):
                            P.op("tensor", lambda e, c=c, j=j, pq=pq, wb=wb: e.matmul(
                                out=pq[:], lhsT=wb[:, c, j * 128:(j + 1) * 128], rhs=hT[:, c, :],
                                start=(c == 0), stop=(c == 15)), reads=[wb, hT], writes=[pq])
                        P.op("scalar", lambda e, j=j, ng=ng, pq=pq: e.activation(out=pqs[:, ng * 4 + j, :], in_=pq[:], func=AF.Copy),
                             reads=[pq], writes=[pqs])
            ering = Ring([P.sbuf("edt%d" % i, [128, 16, 512], BF16) for i in range(2)])
            uring = Ring([P.sbuf("eut%d" % i, [128, 2, 2048], BF16) for i in range(2)])
            ytmp = P.sbuf("ytmp", [128, 1024], F32)
            gsr = Ring([P.sbuf("gS%d" % i, [128, IQ * 128], BF16) for i in range(3)])
            wTr = Ring([P.sbuf("wT%d" % i, [128, IQ, 128], BF16) for i in range(2)])
            G = P.sbuf("G", [128, IQ * 128], BF16)
            Eb = Ring([P.sbuf("Eb%d" % i, [128, IQ * 128], BF16) for i in range(2)])
            E = Ring([P.sbuf("E%d" % i, [128, IQ, 128], F32) for i in range(2)])
            Ap = P.sbuf("Ap", [128, 8, 128], F32)
            Bp = P.sbuf("Bp", [128, 8, 128], F32)
            nb = P.sbuf("nb", [128, 8, 2], F32)
            lz = P.sbuf("lz", [128, 8], F32)
            th = P.sbuf("th", [128, 8], F32)
            sc = P.sbuf("sc", [128, 16, 128], F32)
            sv = P.sbuf("sv", [128, 8, 2, 16], F32)
            tmp1 = P.sbuf("tmp1", [128, 128], F32)
            cand = P.sbuf("cand", [128, 8, 256], F32)
            tmp2 = P.sbuf("tmp2", [128, 256], F32)
            fv = P.sbuf("fv", [128, 8, 16], F32)
            ef = P.sbuf("ef", [128, 8, 16], F32)
            nm = P.sbuf("nm", [128, 8], F32)
            zs = P.sbuf("zs", [128, 8], F32)
            rz = P.sbuf("rz", [128, 8], F32)
            edsrc = self.edT_b[l]
            eusrc = self.eu_b[l]
            sc4 = sc[:, :, :].rearrange("p (h c) k -> p h c k", c=2)

            for b in range(4):
                bs = slice(b * 128, (b + 1) * 128)
                for q4 in range(4):
                    ps = pmm.next()
                    for j in range(4):
                        ci = q4 * 4 + j
                        P.op("tensor", lambda e, ci=ci, j=j, ps=ps: e.matmul(
                            out=ps[:, j * 128:(j + 1) * 128], lhsT=pqs[:, ci, bs], rhs=self.skT[:, ci, :], start=True, stop=True),
                            reads=[pqs, self.skT], writes=[ps])
                    P.op("scalar", lambda e, q4=q4, ps=ps: e.activation(
                        out=sc[:, q4 * 4:(q4 + 1) * 4, :], in_=ps[:, :].rearrange("p (c k) -> p c k", k=128), func=AF.Copy),
                        reads=[ps], writes=[sc])
                for ci in range(16):
                    hh, cc = ci // 2, ci % 2
                    P.op("vector", lambda e, ci=ci, hh=hh, cc=cc: e.max(out=sv[:, hh, cc, 0:8], in_=sc[:, ci, :]), reads=[sc], writes=[sv])
                    P.op("vector", lambda e, ci=ci, hh=hh, cc=cc: e.match_replace(out=tmp1[:], in_to_replace=sv[:, hh, cc, 0:8],
                                                                                   in_values=sc[:, ci, :], imm_value=-1e30),
                         reads=[sc, sv], writes=[tmp1])
                    P.op("vector", lambda e, hh=hh, cc=cc: e.max(out=sv[:, hh, cc, 8:16], in_=tmp1[:]), reads=[tmp1], writes=[sv])
                c4 = cand[:, :, :].rearrange("p h (a b) -> p h a b", b=16)
                in0 = sv[:, :, 0:1, :].rearrange("p h o a -> p h a o").to_broadcast([128, 8, 16, 16])
                in1 = sv[:, :, 1:2, :].to_broadcast([128, 8, 16, 16])
                P.op("vector", lambda e: e.tensor_tensor(out=c4, in0=in0, in1=in1, op=ALU.add), reads=[sv], writes=[cand])
                for hh in range(8):
                    P.op("vector", lambda e, hh=hh: e.max(out=fv[:, hh, 0:8], in_=cand[:, hh, :]), reads=[cand], writes=[fv])
                    P.op("vector", lambda e, hh=hh: e.match_replace(out=tmp2[:], in_to_replace=fv[:, hh, 0:8],
                                                                     in_values=cand[:, hh, :], imm_value=-1e30),
                         reads=[cand, fv], writes=[tmp2])
                    P.op("vector", lambda e, hh=hh: e.max(out=fv[:, hh, 8:16], in_=tmp2[:]), reads=[tmp2], writes=[fv])
                P.op("vector", lambda e: e.tensor_scalar(out=nm[:], in0=fv[:, :, 0], scalar1=-1.0, scalar2=None, op0=ALU.mult),
                     reads=[fv], writes=[nm])
                P.op("vector", lambda e: e.tensor_tensor(out=ef[:], in0=fv[:], in1=nm[:].unsqueeze(2).to_broadcast([128, 8, 16]),
                                                         op=ALU.add), reads=[fv, nm], writes=[ef])
                P.op("scalar", lambda e: e.activation(out=ef[:], in_=ef[:], func=AF.Exp), reads=[ef], writes=[ef])
                P.op("vector", lambda e: e.tensor_reduce(out=zs[:], in_=ef[:], axis=AX.X, op=ALU.add), reads=[ef], writes=[zs])
                P.op("scalar", lambda e: e.activation(out=lz[:], in_=zs[:], func=AF.Ln), reads=[zs], writes=[lz])
                P.op("vector", lambda e: e.tensor_tensor(out=lz[:], in0=nm[:], in1=lz[:], op=ALU.subtract), reads=[nm, lz], writes=[lz])
                P.op("vector", lambda e: e.tensor_scalar(out=nb[:, :, 1], in0=sv[:, :, 1, 0], scalar1=-1.0, scalar2=None, op0=ALU.mult),
                     reads=[sv], writes=[nb])
                P.op("vector", lambda e: e.tensor_tensor(out=nb[:, :, 0], in0=lz[:], in1=sv[:, :, 1, 0], op=ALU.add),
                     reads=[sv, lz, nb], writes=[nb])
                for hh in range(8):
                    P.op("scalar", lambda e, hh=hh: e.activation(out=Ap[:, hh, :], in_=sc4[:, hh, 0, :], func=AF.Exp,
                                                                 bias=nb[:, hh, 0:1], scale=1.0), reads=[sc, nb], writes=[Ap])
                    P.op("scalar", lambda e, hh=hh: e.activation(out=Bp[:, hh, :], in_=sc4[:, hh, 1, :], func=AF.Exp,
                                                                 bias=nb[:, hh, 1:2], scale=1.0), reads=[sc, nb], writes=[Bp])
                P.op("vector", lambda e: e.tensor_tensor(out=th[:], in0=fv[:, :, 15], in1=lz[:], op=ALU.add), reads=[fv, lz], writes=[th])
                P.op("scalar", lambda e: e.activation(out=th[:], in_=th[:], func=AF.Exp, bias=self.negm[:, 0:1], scale=1.0),
                     reads=[th, self.negm], writes=[th])

                NQ = IQ * 128 // 512
                gsl = [None] * NP

                def down_group(pc, q):
                    if q == 0:
                        gsl[pc] = gsr.next()
                    gs = gsl[pc]
                    eg = pc * NQ + q
                    wb = ering.next()
                    P.dma("sync", wb[:], edsrc.t.ap()[eg * 128:(eg + 1) * 128, :].rearrange("p (c n) -> p c n", n=512),
                          reads=[edsrc], writes=[wb])
                    ps = pmm.next()
                    for c in range(16):
                        P.op("tensor", lambda e, c=c, ps=ps, wb=wb: e.matmul(
                            out=ps[:], lhsT=hT[:, c, bs], rhs=wb[:, c, :], start=(c == 0), stop=(c == 15)),
                            reads=[hT, wb], writes=[ps])
                    P.op("scalar", lambda e, q=q, ps=ps, gs=gs: e.activation(
                        out=gs[:, q * 512:(q + 1) * 512], in_=ps[:], func=AF.Gelu_apprx_tanh), reads=[ps], writes=[gs])

                def mask_head(pc, hh):
                    i0 = pc * IQ
                    ee = E.next()
                    P.op("gpsimd", lambda e, ee=ee, hh=hh, i0=i0: e.tensor_tensor(
                        out=ee[:, :, :], in0=Ap[:, hh, i0:i0 + IQ].unsqueeze(2).to_broadcast([128, IQ, 128]),
                        in1=Bp[:, hh:hh + 1, :].to_broadcast([128, IQ, 128]), op=ALU.mult), reads=[Ap, Bp], writes=[ee])
                    ef2 = ee[:, :, :].rearrange("p i j -> p (i j)")
                    if hh == 0:
                        P.op("vector", lambda e, ef2=ef2, hh=hh: e.scalar_tensor_tensor(
                            out=G[:], in0=ef2, scalar=th[:, hh:hh + 1], in1=ef2, op0=ALU.is_ge, op1=ALU.mult),
                            reads=[ee, th], writes=[G])
                    else:
                        eb = Eb.next()
                        P.op("vector", lambda e, ef2=ef2, hh=hh, eb=eb: e.scalar_tensor_tensor(
                            out=eb[:], in0=ef2, scalar=th[:, hh:hh + 1], in1=ef2, op0=ALU.is_ge, op1=ALU.mult),
                            reads=[ee, th], writes=[eb])
                        P.op("vector", lambda e, eb=eb: e.tensor_tensor(out=G[:], in0=G[:], in1=eb[:], op=ALU.add),
                             reads=[eb, G], writes=[G])

                def wmul(pc):
                    gs = gsl[pc]
                    P.op("vector", lambda e, gs=gs: e.tensor_tensor(out=gs[:], in0=gs[:], in1=G[:], op=ALU.mult),
                         reads=[gs, G], writes=[gs])

                def transposes(pc):
                    gs = gsl[pc]
                    wT = wTr.next()
                    for cg in range(IQ // 4):
                        pt = ptr.next()
                        for j in range(4):
                            ch = cg * 4 + j
                            P.op("tensor", lambda e, ch=ch, j=j, pt=pt, gs=gs: e.transpose(
                                out=pt[:, j, :], in_=gs[:, ch * 128:(ch + 1) * 128], identity=self.ident[:]),
                                reads=[gs, self.ident], writes=[pt])
                        P.op("scalar", lambda e, cg=cg, pt=pt, wT=wT: e.activation(out=wT[:, cg * 4:(cg + 1) * 4, :], in_=pt[:, :, :], func=AF.Copy),
                             reads=[pt], writes=[wT])
                    return wT

                def up(pc, wT):
                    i0 = pc * IQ
                    for u in range(IQ // 2):
                        ut = uring.next()
                        e0 = (i0 + u * 2) * 128
                        P.dma("sync", ut[:], eusrc.t.ap()[e0:e0 + 256, :].rearrange("(c p) d -> p c d", p=128),
                              reads=[eusrc], writes=[ut])
                        for j in range(2):
                            ch = u * 2 + j
                            first = (pc == 0 and ch == 0)
                            last = (pc == NP - 1 and ch == IQ - 1)
                            for dg in range(4):
                                P.op("tensor", lambda e, ch=ch, j=j, dg=dg, ut=ut, wT=wT, first=first, last=last: e.matmul(
                                    out=y4[:, dg * 512:(dg + 1) * 512], lhsT=wT[:, ch, :], rhs=ut[:, j, dg * 512:(dg + 1) * 512],
                                    start=first, stop=last), reads=[wT, ut], writes=[y4])

                for q in range(NQ):
                    down_group(0, q)
                for q in range(NQ):
                    down_group(1, q)
                for hh in range(8):
                    mask_head(0, hh)
                wmul(0)
                for pc in range(NP):
                    wT = transposes(pc)
                    up(pc, wT)
                    for hh in range(8):
                        if pc + 1 < NP:
                            mask_head(pc + 1, hh)
                        if hh >= 8 - NQ and pc + 2 < NP:
                            down_group(pc + 2, hh - (8 - NQ))
                    if pc + 1 < NP:
                        wmul(pc + 1)
                for yh in range(2):
                    P.op("scalar", lambda e, yh=yh: e.activation(out=ytmp[:], in_=y4[:, yh * 1024:(yh + 1) * 1024], func=AF.Copy),
                         reads=[y4], writes=[ytmp])
                    P.op("vector", lambda e, b=b, yh=yh: e.tensor_tensor(out=xt[:, b, yh * 1024:(yh + 1) * 1024],
                                                                        in0=xt[:, b, yh * 1024:(yh + 1) * 1024], in1=ytmp[:], op=ALU.add),
                         reads=[xt, ytmp], writes=[xt])


def _rope_tables(pos0):
    pos = np.concatenate([np.arange(pos0[0], pos0[0] + TOK), np.arange(pos0[1], pos0[1] + TOK)]).astype(np.float32)
    inv = (np.float32(10000.0) ** (-np.arange(0, 64, 2, dtype=np.float32) / np.float32(64))).astype(np.float32)
    ang = (pos[:, None] * inv[None, :]).astype(np.float32)
    cos, sin = np.cos(ang).astype(np.float32), np.sin(ang).astype(np.float32)
    p = np.arange(128)
    f = p % 32
    sgn = np.where((p % 64) < 32, -1.0, 1.0).astype(np.float32)
    cosT = np.ascontiguousarray(cos[:, f].T)
    sinT = np.ascontiguousarray((sin[:, f] * sgn[None, :]).T)
    return cosT, sinT


def make_in_maps(inp, exchange="ag"):
    f32 = np.float32
    bf = ml_dtypes.bfloat16
    A = lambda a: np.ascontiguousarray(np.asarray(a, dtype=f32))
    x_all = np.concatenate([A(inp["x_prompt"]).reshape(-1, D), A(inp["x_sample"]).reshape(-1, D)], axis=0)
    p = np.arange(128)
    partner = np.where((p % 64) < 32, p + 32, p - 32)
    rotp = np.zeros((128, 128), f32)
    rotp[partner, p] = 1.0
    shared = {
        "ident": np.eye(128, dtype=f32).astype(bf),
        "rotp": rotp.astype(bf),
        "ones": np.ones((128, 128), f32).astype(bf),
        "w_in": A(inp["w_in"]).reshape(2 * D, INW),
        "w_out": A(inp["w_out"]).reshape(2 * D, D),
        "w_q": A(inp["w_query"]).reshape(2 * D, D),
        "edT": np.ascontiguousarray(A(inp["expert_down"]).reshape(2, 32, 512, 16, 128).transpose(0, 1, 4, 3, 2)).reshape(2 * 32 * 128, 16 * 512),
        "eu": A(inp["expert_up"]).reshape(2 * NE, D),
        "nmix": np.ascontiguousarray(A(inp["norm_mix"]).reshape(2, 16, 128).transpose(2, 0, 1).reshape(128, 32)),
        "nffn": np.ascontiguousarray(A(inp["norm_ffn"]).reshape(2, 16, 128).transpose(2, 0, 1).reshape(128, 32)),
        "nfin": A(inp["norm_final"]).reshape(1, D),
        "lamv": np.ascontiguousarray(np.stack([A(inp["lambda_q1"]), A(inp["lambda_k1"]), A(inp["lambda_q2"]),
                                               A(inp["lambda_k2"])], axis=1).reshape(8, 64)),
        "subln": np.ascontiguousarray(A(inp["subln"]).T),
        "lng": A(inp["gate_ln_g"]),
        "lnb": A(inp["gate_ln_b"]),
        "gon": A(inp["gate_out_norm"]),
        "swT": np.ascontiguousarray(A(inp["spatial_w"]).transpose(0, 1, 3, 2)).reshape(2 * 8 * 128, 128),
        "sbT": np.ascontiguousarray(A(inp["spatial_b"]).transpose(2, 0, 1).reshape(128, 16)),
        "skT": np.ascontiguousarray(A(inp["sub_keys"]).transpose(0, 4, 2, 1, 3)).reshape(2 * 128, 16 * 128),
    }
    tables = {}
    maps = []
    for c in range(NCORES):
        if c < 4:
            partner = c ^ 1
            pos0 = ((c % 2) * TOK, (partner % 2) * TOK)
        else:
            partner = c
            pos0 = (0, 0)
        if pos0 not in tables:
            tables[pos0] = _rope_tables(pos0)
        cosT, sinT = tables[pos0]
        mask = np.zeros((128, 64), f32)
        xc = np.concatenate([x_all[c * TOK:(c + 1) * TOK], x_all[partner * TOK:(partner + 1) * TOK]], axis=0)
        m = dict(shared)
        m.update({"x": np.ascontiguousarray(xc), "cosT": cosT, "sinT": sinT, "maskb": mask})
        maps.append(m)
    return maps


_NC_CACHE = {}


def kernel(**inputs):
    key = "full"
    if key not in _NC_CACHE:
        _NC_CACHE[key] = Builder(exchange="none").build()
    nc = _NC_CACHE[key]
    maps = make_in_maps(inputs, "none")
    res = run_bass_kernel_spmd(nc, maps, core_ids=list(range(NCORES)))
    ys = [np.asarray(r["y"], dtype=np.float32) for r in res.results]
    y_prompt = np.concatenate(ys[:4], axis=0).reshape(2, 8192, D)
    y_sample = np.concatenate(ys[4:], axis=0).reshape(4, 4096, D)
    return (y_prompt, y_sample)
```
